# Optimizing a Trainium2 kernel written in Bass

```python
import math
import jax, jax.numpy as jnp
from jax import lax
import numpy as np

D_MODEL = 1024
BATCH = 8
SEQ = 4096
DEPTH = 2

CHUNK = 64
D_FF = 2816
N_BRANCH = 4
BRANCH_W = D_MODEL // 2
POOL_WINDOWS = (2, 4, 8, 16)
N_POOL_GROUPS = len(POOL_WINDOWS)
POOL_GROUP_W = BRANCH_W // N_POOL_GROUPS
SCONV_K = 3
CCONV_K = 31
SGU_BLOCK = 128
SGU_HEADS = 4
SGU_HEAD_W = BRANCH_W // SGU_HEADS
COLS_A = BRANCH_W
COLS_B = 3 * BRANCH_W
COLS_C = 2 * BRANCH_W
COLS_D = 2 * BRANCH_W
COLS_G = N_BRANCH * D_MODEL
IN_COLS = COLS_A + COLS_B + COLS_C + COLS_D + COLS_G
SPLITS = (COLS_A, COLS_A + COLS_B, COLS_A + COLS_B + COLS_C, COLS_A + COLS_B + COLS_C + COLS_D)
EPS = 1e-6

kernel_name = "hybrid_gated_parallel_mixers_macaron"


def rms_norm(x, g):
    x32 = x.astype(jnp.float32)
    y = x32 * lax.rsqrt(jnp.mean(x32 * x32, axis=-1, keepdims=True) + EPS)
    return (y * g.astype(jnp.float32)).astype(x.dtype)


def layer_norm(x, g, b):
    x32 = x.astype(jnp.float32)
    mu = jnp.mean(x32, axis=-1, keepdims=True)
    var = jnp.mean(jnp.square(x32 - mu), axis=-1, keepdims=True)
    y = (x32 - mu) * lax.rsqrt(var + EPS)
    return (y * g.astype(jnp.float32) + b.astype(jnp.float32)).astype(x.dtype)


def swiglu_half(x, g, w13, w2):
    h = rms_norm(x, g)
    a, b = jnp.split(h @ w13, 2, axis=-1)
    return x + 0.5 * ((jax.nn.silu(a) * b) @ w2)


def causal_dwconv(x, w):
    k, c = w.shape
    return lax.conv_general_dilated(
        x, w[:, None, :].astype(x.dtype), window_strides=(1,), padding=[(k - 1, 0)],
        dimension_numbers=("NWC", "WIO", "NWC"), feature_group_count=c)


def pool_mixer(a, pool_w, pool_scale):
    bn, s, _ = a.shape
    a32 = a.astype(jnp.float32).reshape(bn, s, N_POOL_GROUPS, POOL_GROUP_W)
    cs = jnp.cumsum(a32, axis=1)
    t = jnp.arange(s)
    outs = []
    for gi, win in enumerate(POOL_WINDOWS):
        c = cs[:, :, gi]
        lag = jnp.pad(c, ((0, 0), (win, 0), (0, 0)))[:, :s]
        cnt = jnp.minimum(t + 1, win).astype(jnp.float32)[None, :, None]
        outs.append((c - lag) / cnt - a32[:, :, gi])
    d = jnp.stack(outs, axis=2).astype(a.dtype)
    y = jnp.einsum('bsgc,gcd->bsgd', d, pool_w).reshape(bn, s, BRANCH_W)
    return y * pool_scale


def short_conv_mixer(p, sconv_w):
    xin, bg, cg = jnp.split(p, 3, axis=-1)
    return bg * causal_dwconv(cg * xin, sconv_w)


def conformer_conv_mixer(p, cconv_w, ln_g, ln_b):
    a, b = jnp.split(p, 2, axis=-1)
    y = a * jax.nn.sigmoid(b)
    y = causal_dwconv(y, cconv_w)
    y = layer_norm(y, ln_g, ln_b)
    return jax.nn.silu(y)


def spatial_gating_mixer(p, ln_g, ln_b, sgu_w, sgu_b):
    u, v = jnp.split(jax.nn.gelu(p), 2, axis=-1)
    v = layer_norm(v, ln_g, ln_b)
    bn, s, _ = v.shape
    v = v.reshape(bn, s // SGU_BLOCK, SGU_BLOCK, SGU_HEADS, SGU_HEAD_W)
    i = jnp.arange(SGU_BLOCK)
    mask = (i[None, :] // CHUNK) <= (i[:, None] // CHUNK)
    w = jnp.where(mask[None], sgu_w, jnp.zeros((), sgu_w.dtype))
    z = jnp.einsum('hij,bnjhc->bnihc', w, v) + sgu_b.T[None, None, :, :, None]
    return u * z.reshape(bn, s, BRANCH_W)


def setup_inputs(seed: int = 0) -> dict:
    key = jax.random.key(seed)
    ks = jax.random.split(key, 24)
    f32 = jnp.float32

    def nrm(k, shape, scale):
        return jax.random.normal(k, shape, f32) * scale

    def gain(k, shape):
        return 1.0 + 0.05 * jax.random.normal(k, shape, f32)

    L = DEPTH
    return {
        "x": jax.random.normal(ks[0], (BATCH, SEQ, D_MODEL), f32),
        "ffn1_norm": gain(ks[1], (L, D_MODEL)),
        "ffn1_w13": nrm(ks[2], (L, D_MODEL, 2 * D_FF), D_MODEL ** -0.5),
        "ffn1_w2": nrm(ks[3], (L, D_FF, D_MODEL), D_FF ** -0.5),
        "mix_norm": gain(ks[4], (L, D_MODEL)),
        "w_in": nrm(ks[5], (L, D_MODEL, IN_COLS), D_MODEL ** -0.5),
        "pool_w": nrm(ks[6], (L, N_POOL_GROUPS, POOL_GROUP_W, POOL_GROUP_W), POOL_GROUP_W ** -0.5),
        "pool_scale": gain(ks[7], (L, BRANCH_W)),
        "sconv_w": nrm(ks[8], (L, SCONV_K, BRANCH_W), SCONV_K ** -0.5),
        "cconv_w": nrm(ks[9], (L, CCONV_K, BRANCH_W), CCONV_K ** -0.5),
        "cconv_ln_g": gain(ks[10], (L, BRANCH_W)),
        "cconv_ln_b": nrm(ks[11], (L, BRANCH_W), 0.02),
        "sgu_ln_g": gain(ks[12], (L, BRANCH_W)),
        "sgu_ln_b": nrm(ks[13], (L, BRANCH_W), 0.02),
        "sgu_w": nrm(ks[14], (L, SGU_HEADS, SGU_BLOCK, SGU_BLOCK), SGU_BLOCK ** -0.5),
        "sgu_b": gain(ks[15], (L, SGU_HEADS, SGU_BLOCK)),
        "w_up": nrm(ks[16], (L, N_BRANCH, BRANCH_W, D_MODEL), BRANCH_W ** -0.5),
        "w_out": nrm(ks[17], (L, D_MODEL, D_MODEL), D_MODEL ** -0.5),
        "ffn2_norm": gain(ks[18], (L, D_MODEL)),
        "ffn2_w13": nrm(ks[19], (L, D_MODEL, 2 * D_FF), D_MODEL ** -0.5),
        "ffn2_w2": nrm(ks[20], (L, D_FF, D_MODEL), D_FF ** -0.5),
        "final_norm": gain(ks[21], (D_MODEL,)),
    }


def reference(x, ffn1_norm, ffn1_w13, ffn1_w2, mix_norm, w_in, pool_w, pool_scale,
              sconv_w, cconv_w, cconv_ln_g, cconv_ln_b, sgu_ln_g, sgu_ln_b, sgu_w, sgu_b,
              w_up, w_out, ffn2_norm, ffn2_w13, ffn2_w2, final_norm):
    bn, s, d = x.shape
    for l in range(DEPTH):
        x = swiglu_half(x, ffn1_norm[l], ffn1_w13[l], ffn1_w2[l])
        h = rms_norm(x, mix_norm[l])
        proj = h @ w_in[l]
        pa, pb, pc, pd, pg = jnp.split(proj, SPLITS, axis=-1)
        ya = pool_mixer(pa, pool_w[l], pool_scale[l])
        yb = short_conv_mixer(pb, sconv_w[l])
        yc = conformer_conv_mixer(pc, cconv_w[l], cconv_ln_g[l], cconv_ln_b[l])
        yd = spatial_gating_mixer(pd, sgu_ln_g[l], sgu_ln_b[l], sgu_w[l], sgu_b[l])
        y = jnp.stack([ya, yb, yc, yd], axis=2)
        up = jnp.einsum('bsgc,gcd->bsgd', y, w_up[l])
        gates = jax.nn.sigmoid(pg.reshape(bn, s, N_BRANCH, d))
        merged = jnp.sum(gates * up, axis=2)
        x = x + merged @ w_out[l]
        x = swiglu_half(x, ffn2_norm[l], ffn2_w13[l], ffn2_w2[l])
    return rms_norm(x, final_norm)
```

```python
import numpy as np
from contextlib import ExitStack
import concourse.bass as bass
import concourse.mybir as mybir
from concourse.bass_utils import run_bass_kernel_spmd

F32 = mybir.dt.float32
BF16 = mybir.dt.bfloat16
AF = mybir.ActivationFunctionType
ALU = mybir.AluOpType

D = 1024
S = 4096
DFF = 2816
KC = 8
FC = 22
T = 512
EPS = 1e-6
NSLOT = 5
SLOT_E = 4096
STG_E = 2048
NSTG = 3
DEFER_FRAC = 0.4
SAME_ENG_SYNC = True
BRANCHES = "ABCD"


def _w_in_panel(w_in, c0):
    return w_in[:, c0:c0 + 512].reshape(KC, 128, 512).transpose(1, 0, 2).reshape(128, -1)


def layer_panels(L, inp):
    out = []

    def ffn(tag, w13, w2):
        for g in range(11):
            a = np.stack([w13[:, ab * DFF + g * 256: ab * DFF + (g + 1) * 256] for ab in range(2)], 0)
            a = a.reshape(2, KC, 128, 256).transpose(2, 0, 1, 3).reshape(128, -1)
            out.append((f"{tag}_w13_{g}", a))
        for m in range(8):
            a = w2[:, m * 128:(m + 1) * 128].reshape(FC, 128, 128).transpose(1, 0, 2).reshape(128, -1)
            out.append((f"{tag}_w2_{m}", a))

    ffn("f1", inp["ffn1_w13"][L], inp["ffn1_w2"][L])
    w_in = inp["w_in"][L]
    for nm, c0 in (("Cb", 2560), ("Ca", 2048), ("Bxin", 512), ("Bcg", 1536), ("Bbg", 1024),
                   ("A", 0), ("Du", 3072), ("Dv", 3584)):
        out.append((f"win_{nm}", _w_in_panel(w_in, c0)))
    w_up = inp["w_up"][L]
    for m in range(8):
        a = w_up[:, :, m * 128:(m + 1) * 128].reshape(4, 4, 128, 128).transpose(2, 0, 1, 3).reshape(128, -1)
        out.append((f"up_{m}", a))
        g = np.stack([w_in[:, 4096 + gi * 1024 + m * 128: 4096 + gi * 1024 + (m + 1) * 128] for gi in range(4)], 0)
        g = g.reshape(4, KC, 128, 128).transpose(2, 0, 1, 3).reshape(128, -1)
        out.append((f"gt_{m}", g))
    w_out = inp["w_out"][L]
    for hf in range(2):
        out.append((f"wo_{hf}", _w_in_panel(w_out, hf * 512)))
    ffn("f2", inp["ffn2_w13"][L], inp["ffn2_w2"][L])
    return out


VEC_LAYOUT = [("ffn1_norm", 8), ("mix_norm", 8), ("ffn2_norm", 8), ("pool_scale", 4), ("sconv", 12),
              ("cconv", 124), ("cln_g", 4), ("cln_b", 4)]
VEC_PER_LAYER = sum(n for _, n in VEC_LAYOUT)


def build_host_arrays(inp, NL):
    f = np.float32
    panels = []
    for L in range(NL):
        panels += [(f"L{L}_{n}", a) for n, a in layer_panels(L, inp)]
    offs = {}
    o = 0
    for n, a in panels:
        offs[n] = (o, a.shape[1])
        o += a.shape[1]
    wall = np.ascontiguousarray(np.concatenate([a for _, a in panels], axis=1).astype(f))
    NV = NL * VEC_PER_LAYER + 8
    vecs = np.zeros((128, NV), f)

    def chunked(v):
        return v.reshape(-1, 128).T

    for L in range(NL):
        base = L * VEC_PER_LAYER
        c = base
        for nm, n in VEC_LAYOUT:
            if nm in ("ffn1_norm", "mix_norm", "ffn2_norm", "pool_scale"):
                v = chunked(inp[nm][L])
            elif nm == "sconv":
                v = inp["sconv_w"][L].reshape(3, 4, 128).transpose(2, 0, 1).reshape(128, 12)
            elif nm == "cconv":
                v = inp["cconv_w"][L].reshape(31, 4, 128).transpose(2, 0, 1).reshape(128, 124)
            elif nm == "cln_g":
                v = chunked(inp["cconv_ln_g"][L])
            elif nm == "cln_b":
                v = chunked(inp["cconv_ln_b"][L])
            vecs[:, c:c + n] = v
            c += n
    vecs[:, NL * VEC_PER_LAYER:] = chunked(inp["final_norm"])
    bc = np.zeros((128, NL, 2, 512), f)
    wsm = np.zeros((128, NL, 2, 512), f)
    sgub = np.zeros((1, NL * 512), f)
    for L in range(NL):
        bc[:, L, 0, :] = inp["sgu_ln_g"][L][None, :]
        bc[:, L, 1, :] = inp["sgu_ln_b"][L][None, :]
        wsm[:, L, 0, :] = inp["pool_w"][L].transpose(1, 0, 2).reshape(128, 512)
        wsm[:, L, 1, :] = inp["sgu_w"][L].transpose(2, 0, 1).reshape(128, 512)
        sgub[0, L * 512:(L + 1) * 512] = inp["sgu_b"][L].reshape(512)
    cnt = np.zeros((128, 4, 16), f)
    for gi, w in enumerate((2, 4, 8, 16)):
        cnt[:, gi, :] = (1.0 / np.minimum(np.arange(16) + 1, w))[None, :]
    ident = np.eye(128, dtype=f)
    return dict(wall=wall, vecs=vecs, bc=bc, wsm=wsm, sgub=sgub, cnt=cnt, ident=ident), offs


class Tok:
    __slots__ = ("sem", "val", "eng")

    def __init__(self, sem, val, eng):
        self.sem, self.val, self.eng = sem, val, eng


class Buf:
    __slots__ = ("name", "w", "r")

    def __init__(self, name):
        self.name, self.w, self.r = name, None, {}


class Eng:
    def __init__(self, name, sem):
        self.name, self.sem = name, sem
        self.count = 0
        self.known = {}
        self.q = []
        self.pending = []


class DmaSem:
    def __init__(self, sem):
        self.sem, self.count = sem, 0


class _ArenaView:
    def __init__(self, bf):
        self.bf = bf

    def __getitem__(self, idx):
        p, u, f = idx
        assert isinstance(u, int)
        lo = 0 if f.start is None else f.start
        hi = 512 if f.stop is None else f.stop
        base = (u % 2) * 512
        return self.bf[p, u // 2, base + lo:base + hi]


class Prog:
    def __init__(self, nc, es):
        self.nc, self.es = nc, es
        self.eng = {}
        for n in ("pe", "act", "dve", "pool", "sp"):
            self.eng[n] = Eng(n, es.enter_context(nc.semaphore("sem_" + n)))
        self.n_wait = 0
        self.n_ins = 0

    def dsem(self, name):
        return DmaSem(self.es.enter_context(self.nc.semaphore(name)))

    def _waits(self, E, needs):
        best = {}
        for t in needs:
            if t is None:
                continue
            if t.eng is E and (E.name == "pe" or not SAME_ENG_SYNC):
                continue
            assert t.val is not None, "dependency on an unresolved (unsignalled) token"
            k = id(t.sem)
            if k not in best or best[k].val < t.val:
                best[k] = t
        for k, t in best.items():
            if E.known.get(k, 0) >= t.val:
                continue
            E.known[k] = t.val
            E.q.append(lambda e, s=t.sem, v=t.val: e.wait_ge(s, v))
            self.n_wait += 1

    def _needs(self, reads, writes):
        needs = []
        for b in reads:
            needs.append(b.w)
        for b in writes:
            needs.append(b.w)
            needs.extend(b.r.values())
        return needs

    def _commit(self, tok, key, reads, writes):
        for b in reads:
            b.r[key] = tok
        for b in writes:
            b.w = tok
            b.r = {}

    def op(self, en, fn, reads=(), writes=(), signal=True):
        E = self.eng[en]
        self._waits(E, self._needs(reads, writes))
        self.n_ins += 1
        if signal:
            E.count += 1
            tok = Tok(E.sem, E.count, E)
            for p in E.pending:
                p.val = E.count
            E.pending = []
            E.q.append(lambda e, fn=fn, s=E.sem: fn(e).then_inc(s, 1))
        else:
            tok = Tok(E.sem, None, E)
            E.pending.append(tok)
            E.q.append(lambda e, fn=fn: fn(e))
        self._commit(tok, id(E.sem), reads, writes)
        return tok

    def dma(self, qn, ds, pairs, reads=(), writes=()):
        E = self.eng[qn]
        self._waits(E, self._needs(reads, writes))
        for (o, i) in pairs:
            ds.count += 16
            E.q.append(lambda e, o=o, i=i, s=ds.sem: e.dma_start(out=o, in_=i).then_inc(s, 16))
            self.n_ins += 1
        tok = Tok(ds.sem, ds.count, None)
        self._commit(tok, id(ds.sem), reads, writes)
        return tok

    def wait_all(self, en, toks):
        self._waits(self.eng[en], toks)

    def replay(self, block):
        def mk(name):
            q = self.eng[name].q

            def run(e):
                for f in q:
                    f(e)
            return run
        block.tensor(mk("pe"))
        block.scalar(mk("act"))
        block.vector(mk("dve"))
        block.gpsimd(mk("pool"))
        block.sync(mk("sp"))


class Builder:
    def __init__(self, NT, NL, offs, wtot, stages=("f1", "mix", "f2"), fin=True):
        self.NT, self.NL, self.offs, self.wtot = NT, NL, offs, wtot
        self.stages, self.fin = stages, fin
        nc = self.nc = bass.Bass("TRN2", target_bir_lowering=False)
        NV = NL * VEC_PER_LAYER + 8
        self.NV = NV
        SS = NT * T
        self.d_x = nc.dram_tensor("xT", [D, SS], F32, kind="ExternalInput").ap()
        self.d_o = nc.dram_tensor("oT", [D, SS], F32, kind="ExternalOutput").ap()
        self.d_wall = nc.dram_tensor("wall", [128, wtot], F32, kind="ExternalInput").ap()
        self.d_vecs = nc.dram_tensor("vecs", [128, NV], F32, kind="ExternalInput").ap()
        self.d_bc = nc.dram_tensor("bc", [128, NL, 2, 512], F32, kind="ExternalInput").ap()
        self.d_wsm = nc.dram_tensor("wsm", [128, NL, 2, 512], F32, kind="ExternalInput").ap()
        self.d_sgub = nc.dram_tensor("sgub", [1, NL * 512], F32, kind="ExternalInput").ap()
        self.d_cnt = nc.dram_tensor("cnt", [128, 4, 16], F32, kind="ExternalInput").ap()
        self.d_ident = nc.dram_tensor("ident", [128, 128], F32, kind="ExternalInput").ap()
        self.d_wbf = nc.dram_tensor("wbf", [128, wtot], BF16, kind="Internal").ap()

    def sb(self, name, shape, dt):
        return self.es.enter_context(self.nc.sbuf_tensor(name, shape, dt))

    def build(self):
        nc = self.nc
        NL = self.NL
        with ExitStack() as es:
            self.es = es
            P = self.P = Prog(nc, es)
            self.X = self.sb("X", [128, KC, T], F32)
            self.Xb = [Buf(f"X{k}") for k in range(KC)]
            self.H = self.sb("H", [128, KC, T], BF16)
            self.Hb = [Buf(f"H{k}") for k in range(KC)]
            NU = 80
            self.AR32 = self.sb("AR", [128, NU // 2, T], F32)
            self.ARbf = self.AR32[:, :, :].bitcast(BF16)
            self.AR = _ArenaView(self.ARbf)
            self.ARb = [Buf(f"U{k}") for k in range(NU)]
            self.ring = [self.sb(f"ring{i}", [128, SLOT_E], BF16) for i in range(NSLOT)]
            self.ringb = [Buf(f"ring{i}") for i in range(NSLOT)]
            self.ring_ld = [P.dsem(f"ringld{i}") for i in range(NSLOT)]
            self.ring_st = [P.dsem(f"ringst{i}") for i in range(NSLOT)]
            self.stg = [self.sb(f"stg{i}", [128, STG_E], F32) for i in range(NSTG)]
            self.stgb = [Buf(f"stg{i}") for i in range(NSTG)]
            self.stg_ld = [P.dsem(f"stgld{i}") for i in range(NSTG)]
            self.Aw = self.sb("Aw", [128, 4, 16 + T], F32)
            self.Bw = self.sb("Bw", [128, 4, 2 + T], BF16)
            self.Cw = self.sb("Cw", [128, 4, 30 + T], BF16)
            self.Awb = [Buf(f"Aw{c}") for c in range(4)]
            self.Bwb = [Buf(f"Bw{c}") for c in range(4)]
            self.Cwb = [Buf(f"Cw{c}") for c in range(4)]
            self.hA = self.sb("hA", [128, NL, 4, 16], F32)
            self.hB = self.sb("hB", [128, NL, 4, 2], BF16)
            self.hC = self.sb("hC", [128, NL, 4, 30], BF16)
            self.hAb = [[Buf(f"hA{l}{c}") for c in range(4)] for l in range(NL)]
            self.hBb = [[Buf(f"hB{l}{c}") for c in range(4)] for l in range(NL)]
            self.hCb = [[Buf(f"hC{l}{c}") for c in range(4)] for l in range(NL)]
            self.vecs = self.sb("vecs_sb", [128, self.NV], F32)
            self.bc = self.sb("bc_sb", [128, NL, 2, 512], F32)
            self.poolw = self.sb("poolw", [128, NL, 512], BF16)
            self.sguw = self.sb("sguw", [128, NL, 512], BF16)
            self.bhl = self.sb("bhl", [1, 2, NL * 512], BF16)
            self.cnt = self.sb("cnt_sb", [128, 4, 16], F32)
            self.id32 = self.sb("id32", [128, 128], F32)
            self.idb = self.sb("idb", [128, 128], BF16)
            self.ones32 = self.sb("ones32", [128, 128], F32)
            self.onesb = self.sb("onesb", [128, 128], BF16)
            ND = 8
            self.diag = self.sb("diag", [128, ND, 128], BF16)
            self.diagb = [Buf(f"diag{i}") for i in range(ND)]
            self.diag_i = 0
            self.mhalf = self.sb("mhalf", [128, 2], F32)
            self.small = self.sb("small", [128, 96], F32)
            self.smallb = [Buf(f"small{i}") for i in range(5)]
            self.constb = Buf("consts")
            self.wbfb = {}
            self.ps = [es.enter_context(nc.psum_tensor(f"ps{i}", [128, T], F32)) for i in range(8)]
            self.psb = [Buf(f"ps{i}") for i in range(8)]
            self.ps_i = 0
            self.ds_x = [P.dsem(f"ds_x{k}") for k in range(KC)]
            self.ds_o = [P.dsem(f"ds_o{k}") for k in range(KC)]
            self.ds_c = P.dsem("ds_c")
            self.ds_dbg = P.dsem("ds_dbg")
            self.plan = []
            for L in range(NL):
                self.plan += [f"L{L}_{n}" for n in self.layer_order()]
            self.fetched = 0
            self.used = 0
            self.released = 0
            self.deferred = []
            self.cast_flip = 0

            self.emit_all()
            block = es.enter_context(nc.Block())
            P.replay(block)
        return nc

    def layer_order(self):
        o = []
        if "f1" in self.stages:
            o += [f"f1_w13_{g}" for g in range(11)] + [f"f1_w2_{m}" for m in range(8)]
        if "mix" in self.stages:
            o += ["win_Cb", "win_Ca", "win_Dv", "win_A", "win_Bxin", "win_Bcg", "win_Bbg", "win_Du"]
            for m in range(8):
                o += [f"up_{m}", f"gt_{m}"]
            o += ["wo_0", "wo_1"]
        if "f2" in self.stages:
            o += [f"f2_w13_{g}" for g in range(11)] + [f"f2_w2_{m}" for m in range(8)]
        return o

    def bank(self):
        i = self.ps_i
        self.ps_i = (i + 1) % 6
        return self.ps[i], self.psb[i]

    def vcol(self, L, name, j=0):
        c = L * VEC_PER_LAYER
        for nm, n in VEC_LAYOUT:
            if nm == name:
                return self.vecs[:, c + j:c + j + 1]
            c += n
        raise KeyError(name)

    def _emit_fetch(self, gi):
        P = self.P
        name = self.plan[gi % len(self.plan)]
        tile_i = gi // len(self.plan)
        off, n = self.offs[name]
        s = gi % NSLOT
        slot, sbuf = self.ring[s], self.ringb[s]
        if name not in self.wbfb:
            self.wbfb[name] = Buf("wbf_" + name)
        wb = self.wbfb[name]
        pi = gi % len(self.plan)
        deferp = self.NT >= 2 and pi >= int(len(self.plan) * (1.0 - DEFER_FRAC))
        fp32_path = (tile_i == 0) or (tile_i == 1 and deferp)
        do_store = (tile_i == 0 and not deferp) or (tile_i == 1 and deferp) or self.NT < 2
        if fp32_path:
            while len(self.deferred) > (1 if do_store else 0):
                self._flush_store()
            o = 0
            while o < n:
                ln = min(STG_E, n - o)
                h = self.cast_flip % NSTG
                ce = "dve" if self.cast_flip % 2 == 0 else "act"
                self.cast_flip += 1
                P.dma("sp", self.stg_ld[h], [(self.stg[h][:, 0:ln], self.d_wall[:, off + o:off + o + ln])],
                      writes=[self.stgb[h]])
                if ce == "dve":
                    P.op("dve", lambda e, a=slot[:, o:o + ln], b=self.stg[h][:, 0:ln]: e.tensor_copy(out=a, in_=b),
                         reads=[self.stgb[h]], writes=[sbuf])
                else:
                    P.op("act", lambda e, a=slot[:, o:o + ln], b=self.stg[h][:, 0:ln]: e.activation(out=a, in_=b, func=AF.Copy),
                         reads=[self.stgb[h]], writes=[sbuf])
                o += ln
            if do_store:
                self.deferred.append((s, off, n, wb))
        else:
            while self.deferred:
                self._flush_store()
            P.dma("sp", self.ring_ld[s], [(slot[:, 0:n], self.d_wbf[:, off:off + n])], reads=[wb], writes=[sbuf])

    def _flush_store(self):
        s, off, n, wb = self.deferred.pop(0)
        self.P.dma("sp", self.ring_st[s], [(self.d_wbf[:, off:off + n], self.ring[s][:, 0:n])],
                   reads=[self.ringb[s]], writes=[wb])

    def W(self, name):
        gi = self.used
        assert self.plan[gi % len(self.plan)] == name, (self.plan[gi % len(self.plan)], name)
        while self.fetched <= gi:
            assert self.fetched - NSLOT < self.released, "ring too small for live panels"
            self._emit_fetch(self.fetched)
            self.fetched += 1
        self.used += 1
        s = gi % NSLOT
        return self.ring[s], self.ringb[s]

    def release(self, n=1):
        self.released += n
        assert self.released <= self.used
        self.prefetch()

    def prefetch(self):
        total = len(self.plan) * self.NT
        while self.fetched < min(total, self.released + NSLOT):
            self._emit_fetch(self.fetched)
            self.fetched += 1

    def mm(self, out, outb, lhsT, rhs, reads, start, stop, sig=None):
        self.P.op("pe", lambda e: e.matmul(out, lhsT=lhsT, rhs=rhs, start=start, stop=stop),
                  reads=reads, writes=[outb], signal=(stop if sig is None else sig))

    def act(self, out, in_, func, reads, writes, scale=1.0, bias=0.0):
        self.P.op("act", lambda e: e.activation(out=out, in_=in_, func=func, scale=scale, bias=bias),
                  reads=reads, writes=writes)

    def tt(self, en, out, in0, in1, op, reads, writes):
        self.P.op(en, lambda e: e.tensor_tensor(out=out, in0=in0, in1=in1, op=op), reads=reads, writes=writes)

    def stt(self, out, in0, scalar, op0, in1, op1, reads, writes):
        self.P.op("dve", lambda e: e.scalar_tensor_tensor(out=out, in0=in0, scalar=scalar, op0=op0, in1=in1, op1=op1),
                  reads=reads, writes=writes)

    def ts(self, en, out, in0, s1, op0, reads, writes, s2=None, op1=ALU.bypass):
        self.P.op(en, lambda e: e.tensor_scalar(out=out, in0=in0, scalar1=s1, op0=op0, scalar2=s2, op1=op1),
                  reads=reads, writes=writes)

    def _u32(self, u):
        return self.AR32[:, u // 2, :]

    def emit_all(self):
        P, nc = self.P, self.nc
        NL = self.NL
        cb = self.constb
        P.dma("sp", self.ds_c, [(self.vecs[:, :], self.d_vecs), (self.id32[:, :], self.d_ident)], writes=[cb])
        P.op("act", lambda e: e.activation(out=self.idb[:, :], in_=self.id32[:, :], func=AF.Copy), reads=[cb], writes=[cb])
        P.op("pool", lambda e: e.memset(self.onesb[:, :], 1.0), writes=[cb])
        P.op("pool", lambda e: e.memset(self.ones32[:, :], 1.0), writes=[cb])
        P.op("pool", lambda e: e.memset(self.mhalf[:, :], -0.5), writes=[cb])
        for en in ("pe", "act", "dve", "pool"):
            P.wait_all(en, [cb.w] + list(cb.r.values()))
        xv = self.d_x.rearrange("(k p) s -> p k s", p=128)
        for kc in range(KC):
            P.dma("sp", self.ds_x[kc], [(self.X[:, kc, :], xv[:, kc, 0:T])], writes=[self.Xb[kc]])
        self.prefetch()
        cb2 = self.constb2 = Buf("consts2")
        P.dma("sp", self.ds_c, [(self.bc[:, :, :, :], self.d_bc), (self.cnt[:, :, :], self.d_cnt)], writes=[cb2])
        for L in range(NL):
            for w in range(2):
                h = w
                P.dma("sp", self.stg_ld[h], [(self.stg[h][:, 0:512], self.d_wsm[:, L, w, :])], writes=[self.stgb[h]])
                dst = (self.poolw if w == 0 else self.sguw)[:, L, :]
                P.op("act", lambda e, a=dst, b=self.stg[h][:, 0:512]: e.activation(out=a, in_=b, func=AF.Copy),
                     reads=[self.stgb[h]], writes=[cb2])
            for hh in range(4):
                P.op("pool", lambda e, a=self.sguw[64:128, L, hh * 128:hh * 128 + 64]: e.memset(a, 0.0), writes=[cb2])
        nb_ = NL * 512
        b32 = self.stg[0][0:1, 0:nb_]
        hi32 = self.stg[0][0:1, nb_:2 * nb_]
        P.dma("sp", self.stg_ld[0], [(b32, self.d_sgub)], writes=[self.stgb[0]])
        P.op("act", lambda e: e.activation(out=self.bhl[0:1, 0, :], in_=b32, func=AF.Copy), reads=[self.stgb[0]], writes=[cb2])
        P.op("act", lambda e: e.activation(out=hi32, in_=self.bhl[0:1, 0, :], func=AF.Copy), reads=[cb2], writes=[self.stgb[0]])
        P.op("dve", lambda e: e.tensor_tensor(out=self.bhl[0:1, 1, :], in0=b32, in1=hi32, op=ALU.subtract),
             reads=[self.stgb[0]], writes=[cb2])
        for L in range(NL):
            P.op("pool", lambda e, a=self.hA[:, L, :, :]: e.memset(a, 0.0), writes=self.hAb[L])
            P.op("pool", lambda e, a=self.hB[:, L, :, :]: e.memset(a, 0.0), writes=self.hBb[L])
            P.op("pool", lambda e, a=self.hC[:, L, :, :]: e.memset(a, 0.0), writes=self.hCb[L])
        self.c2_waited = False

        ov = self.d_o.rearrange("(k p) s -> p k s", p=128)
        full = self.fin and tuple(self.stages) == ("f1", "mix", "f2")
        self.h_ready = False
        XP = 40

        def xp(kc):
            return self._u32(XP + 2 * kc), self.ARb[XP + 2 * kc:XP + 2 * kc + 2]

        def next_h():
            ps, psb = self.ps[7], self.psb[7]
            for kc in range(KC):
                u = 36 + (kc % 4)
                sq, sqb = self.AR[:, u, :], [self.ARb[u]]
                xa, xb = xp(kc)
                self.act(sq, xa, AF.Square, reads=xb, writes=sqb)
                self.mm(ps[:, :], psb, self.onesb[:, :], sq, reads=sqb, start=(kc == 0), stop=(kc == KC - 1), sig=True)
            self.sqrt_table_warm()
            r, rb = self._u32(34), self.ARb[34:36]
            self.act(r, ps[:, :], AF.Sqrt, reads=[psb], writes=rb, scale=1.0 / D, bias=EPS)
            P.op("dve", lambda e: e.reciprocal(out=r, in_=r), reads=rb, writes=rb)
            for kc in range(KC):
                xa, xb = xp(kc)
                self.stt(self.H[:, kc, :], xa, self.vcol(0, "ffn1_norm", kc), ALU.mult, r, ALU.mult,
                         reads=xb + rb, writes=[self.Hb[kc]])
            self.h_ready = True

        for tt in range(self.NT):
            if (tt > 0 and not full):
                for kc in range(KC):
                    P.dma("sp", self.ds_x[kc], [(self.X[:, kc, :], xv[:, kc, tt * T:(tt + 1) * T])], writes=[self.Xb[kc]])
            pre = full and tt + 1 < self.NT
            for L in range(NL):
                if "f1" in self.stages:
                    self.ffn(L, "f1", "ffn1_norm")
                if "mix" in self.stages:
                    self.mixer(L, tt)
                if "f2" in self.stages:
                    if pre and L == NL - 1:
                        for kc in range(KC):
                            xa, xb = xp(kc)
                            P.dma("sp", self.ds_x[kc], [(xa, xv[:, kc, (tt + 1) * T:(tt + 2) * T])], writes=xb)
                        self.ffn(L, "f2", "ffn2_norm", hook=next_h)
                    else:
                        self.ffn(L, "f2", "ffn2_norm")
            if self.fin:
                self.final_norm()
            for kc in range(KC):
                P.dma("sp", self.ds_o[kc], [(ov[:, kc, tt * T:(tt + 1) * T], self.X[:, kc, :])], reads=[self.Xb[kc]])
            if pre:
                for kc in range(KC):
                    xa, xb = xp(kc)
                    P.op("pool", lambda e, a=self.X[:, kc, :], b=xa: e.tensor_copy(out=a, in_=b), reads=xb, writes=[self.Xb[kc]])
        while self.deferred:
            self._flush_store()
        toks = [Tok(d.sem, d.count, None) for d in self.ds_o]
        for d in self.ring_st:
            if d.count:
                toks.append(Tok(d.sem, d.count, None))
        P.wait_all("sp", toks)

    def xstat_feed(self, kc):
        ps, psb = self.ps[6], self.psb[6]
        u = 36 + (kc % 4)
        sq, sqb = self.AR[:, u, :], [self.ARb[u]]
        self.act(sq, self.X[:, kc, :], AF.Square, reads=[self.Xb[kc]], writes=sqb)
        self.mm(ps[:, :], psb, self.onesb[:, :], sq, reads=sqb, start=(kc == 0), stop=(kc == KC - 1), sig=True)
        if kc == KC - 1:
            self.xs_ready = True
            self.sqrt_table_warm()

    def sqrt_table_warm(self):
        c = self.small[:, 88:89]
        self.P.op("act", lambda e: e.activation(out=c, in_=self.mhalf[:, 1:2], func=AF.Sqrt, scale=-1.0), writes=[self.smallb[4]])

    def rms_stats(self):
        ps, psb = self.ps[6], self.psb[6]
        if not getattr(self, "xs_ready", False):
            for kc in range(KC):
                self.xstat_feed(kc)
        self.xs_ready = False
        r, rb = self._u32(32), self.ARb[32:34]
        self.act(r, ps[:, :], AF.Sqrt, reads=[psb], writes=rb, scale=1.0 / D, bias=EPS)
        self.P.op("dve", lambda e: e.reciprocal(out=r, in_=r), reads=rb, writes=rb)
        return r, rb

    def rmsnorm_to_H(self, L, gname):
        r, rb = self.rms_stats()
        for kc in range(KC):
            self.stt(self.H[:, kc, :], self.X[:, kc, :], self.vcol(L, gname, kc), ALU.mult, r, ALU.mult,
                     reads=[self.Xb[kc]] + rb, writes=[self.Hb[kc]])

    def final_norm(self):
        r, rb = self.rms_stats()
        c0 = self.NL * VEC_PER_LAYER
        for kc in range(KC):
            self.stt(self.X[:, kc, :], self.X[:, kc, :], self.vecs[:, c0 + kc:c0 + kc + 1], ALU.mult, r, ALU.mult,
                     reads=[self.Xb[kc]] + rb, writes=[self.Xb[kc]])

    def ffn(self, L, tag, gname, hook=None):
        if self.h_ready:
            self.h_ready = False
        else:
            self.rmsnorm_to_H(L, gname)
        for g in range(11):
            slot, sbuf = self.W(f"L{L}_{tag}_w13_{g}")
            pre = None
            if g == 0:
                pre = [self.bank() for _ in range(4)]
                for kc in range(KC):
                    for idx, (p_, pb_) in enumerate(pre):
                        o = (idx % 2) * 2048 + kc * 256 + (idx // 2) * 128
                        self.mm(p_[:, :], pb_, slot[:, o:o + 128], self.H[:, kc, :], reads=[sbuf, self.Hb[kc]],
                                start=(kc == 0), stop=(kc == KC - 1))
            for jj in range(2):
                j = 2 * g + jj
                if pre is not None:
                    (pa, pab), (pb, pbb) = pre[2 * jj], pre[2 * jj + 1]
                else:
                    pa, pab = self.bank()
                    for kc in range(KC):
                        o = kc * 256 + jj * 128
                        self.mm(pa[:, :], pab, slot[:, o:o + 128], self.H[:, kc, :], reads=[sbuf, self.Hb[kc]],
                                start=(kc == 0), stop=(kc == KC - 1))
                    pb, pbb = self.bank()
                    for kc in range(KC):
                        o = 2048 + kc * 256 + jj * 128
                        self.mm(pb[:, :], pbb, slot[:, o:o + 128], self.H[:, kc, :], reads=[sbuf, self.Hb[kc]],
                                start=(kc == 0), stop=(kc == KC - 1))
                u = 24 + 2 * (j % 2)
                sa, sab = self._u32(u), self.ARb[u:u + 2]
                self.act(sa, pa[:, :], AF.Silu, reads=[pab], writes=sab)
                self.tt("dve", self.AR[:, j, :], pb[:, :], sa, ALU.mult, reads=[pbb] + sab, writes=[self.ARb[j]])
            self.release()
        for m in range(KC):
            slot, sbuf = self.W(f"L{L}_{tag}_w2_{m}")
            po, pob = self.bank()
            for j in range(FC):
                self.mm(po[:, :], pob, slot[:, j * 128:(j + 1) * 128], self.AR[:, j, :], reads=[sbuf, self.ARb[j]],
                        start=(j == 0), stop=(j == FC - 1))
            self.release()
            self.stt(self.X[:, m, :], po[:, :], 0.5, ALU.mult, self.X[:, m, :], ALU.add,
                     reads=[pob, self.Xb[m]], writes=[self.Xb[m]])
            if m >= 1:
                self.xstat_feed(m - 1)
        if hook is not None:
            hook()
        self.xstat_feed(KC - 1)

    def mk_diag(self, col_ap):
        i = self.diag_i
        self.diag_i = (i + 1) % len(self.diagb)
        d, db = self.diag[:, i, :], self.diagb[i]
        self.ts("dve", d, self.id32[:, :], col_ap, ALU.mult, reads=[], writes=[db])
        return d, db

    def proj(self, slot, sbuf, c):
        p, pb = self.bank()
        for kc in range(KC):
            o = kc * 512 + c * 128
            self.mm(p[:, :], pb, slot[:, o:o + 128], self.H[:, kc, :], reads=[sbuf, self.Hb[kc]],
                    start=(kc == 0), stop=(kc == KC - 1))
        return p, pb

    def mixer(self, L, tt):
        P = self.P
        AR, ARb = self.AR, self.ARb
        if not self.c2_waited:
            self.c2_waited = True
            cb2 = self.constb2
            for en in ("pe", "act", "dve", "pool"):
                P.wait_all(en, [cb2.w] + list(cb2.r.values()))
        self.rmsnorm_to_H(L, "mix_norm")
        yA, yB, yC, yD = 0, 4, 8, 12
        MERG = 16
        BG, YC32, UU, VLN, DD, TG = 46, 52, 60, 64, 72, 72
        MEAN, VAR, MR = 40, 42, 44
        s1, s1b = self.ps[6], self.psb[6]
        s2, s2b = self.ps[7], self.psb[7]

        for c in range(4):
            P.op("pool", lambda e, a=self.Cw[:, c, 0:30], b=self.hC[:, L, c, :]: e.tensor_copy(out=a, in_=b),
                 reads=[self.hCb[L][c]], writes=[self.Cwb[c]])
        for c in range(4):
            P.op("pool", lambda e, a=self.Bw[:, c, 0:2], b=self.hB[:, L, c, :]: e.tensor_copy(out=a, in_=b),
                 reads=[self.hBb[L][c]], writes=[self.Bwb[c]])
        for c in range(4):
            P.op("pool", lambda e, a=self.Aw[:, c, 0:16], b=self.hA[:, L, c, :]: e.tensor_copy(out=a, in_=b),
                 reads=[self.hAb[L][c]], writes=[self.Awb[c]])
        sCb, bCb = self.W(f"L{L}_win_Cb")
        sCa, bCa = self.W(f"L{L}_win_Ca")
        cbk = {}
        for c0 in (0, 2):
            for c in (c0, c0 + 1):
                cbk[("b", c)] = self.bank()
                cbk[("a", c)] = self.bank()
            for kc in range(KC):
                for c in (c0, c0 + 1):
                    for nm, slot_, sb_ in (("b", sCb, bCb), ("a", sCa, bCa)):
                        p_, pb2 = cbk[(nm, c)]
                        o = kc * 512 + c * 128
                        self.mm(p_[:, :], pb2, slot_[:, o:o + 128], self.H[:, kc, :], reads=[sb_, self.Hb[kc]],
                                start=(kc == 0), stop=(kc == KC - 1))
            for c in (c0, c0 + 1):
                pb_, pbb = cbk[("b", c)]
                u = 24 + 2 * (c % 2)
                sg, sgb = self._u32(u), ARb[u:u + 2]
                self.act(sg, pb_[:, :], AF.Sigmoid, reads=[pbb], writes=sgb)
                pa_, pab = cbk[("a", c)]
                self.tt("dve", self.Cw[:, c, 30:30 + T], pa_[:, :], sg, ALU.mult, reads=[pab] + sgb, writes=[self.Cwb[c]])
                P.op("pool", lambda e, a=self.hC[:, L, c, :], b=self.Cw[:, c, T:T + 30]: e.tensor_copy(out=a, in_=b),
                     reads=[self.Cwb[c]], writes=[self.hCb[L][c]])
        self.release(2)
        sDv, bDv = self.W(f"L{L}_win_Dv")
        for nb in range(4):
            pv, pvb = self.bank()
            for kc in range(KC):
                self.mm(pv[:, :], pvb, self.H[:, kc, nb * 128:(nb + 1) * 128], sDv[:, kc * 512:(kc + 1) * 512],
                        reads=[bDv, self.Hb[kc]], start=(kc == 0), stop=(kc == KC - 1))
            u = (68, 70, 50, 76)[nb]
            gv, gvb = self._u32(u), ARb[u:u + 2]
            self.act(gv, pv[:, :], AF.Gelu_apprx_tanh, reads=[pvb], writes=gvb)
            sm, smb = self.small[:, nb * 16:nb * 16 + 16], self.smallb[nb]
            P.op("dve", lambda e, o=sm[:, 0:6], i=gv: e.bn_stats(out=o, in_=i), reads=gvb, writes=[smb])
            P.op("dve", lambda e, o=sm[:, 6:8], i=sm[:, 0:6]: e.bn_aggr(out=o, in_=i), reads=[smb], writes=[smb])
            self.ts("pool", sm[:, 8:9], sm[:, 7:8], EPS, ALU.add, reads=[smb], writes=[smb])
            self.tt("pool", sm[:, 9:10], sm[:, 8:9], self.mhalf[:, 0:1], ALU.pow, reads=[smb], writes=[smb])
            self.tt("pool", sm[:, 10:11], sm[:, 6:7], sm[:, 9:10], ALU.mult, reads=[smb], writes=[smb])
            self.tt("pool", sm[:, 11:12], sm[:, 10:11], self.mhalf[:, 0:1], ALU.mult, reads=[smb], writes=[smb])
            self.tt("pool", sm[:, 12:13], sm[:, 11:12], sm[:, 11:12], ALU.add, reads=[smb], writes=[smb])

        def dv_finish():
            for nb in range(4):
                u = (68, 70, 50, 76)[nb]
                gv, gvb = self._u32(u), ARb[u:u + 2]
                sm, smb = self.small[:, nb * 16:nb * 16 + 16], self.smallb[nb]
                self.act(gv, gv, AF.Identity, reads=gvb + [smb], writes=gvb, scale=sm[:, 9:10], bias=sm[:, 12:13])
                self.tt("pool", gv, gv, self.bc[:, L, 0, :], ALU.mult, reads=gvb, writes=gvb)
                self.tt("pool", AR[:, VLN + nb, :], gv, self.bc[:, L, 1, :], ALU.add, reads=gvb, writes=[ARb[VLN + nb]])
        self.release()
        sA, bA = self.W(f"L{L}_win_A")
        for c in range(4):
            pa_, pab = self.proj(sA, bA, c)
            self.act(self.Aw[:, c, 16:16 + T], pa_[:, :], AF.Copy, reads=[pab], writes=[self.Awb[c]])
            P.op("pool", lambda e, a=self.hA[:, L, c, :], b=self.Aw[:, c, T:T + 16]: e.tensor_copy(out=a, in_=b),
                 reads=[self.Awb[c]], writes=[self.hAb[L][c]])
        self.release()
        dv_finish()
        def a_lin(c):
            dsl, dslb = AR[:, DD + c, :], ARb[DD + c]
            p2, p2b = self.bank()
            self.mm(p2[:, :], p2b, self.poolw[:, L, c * 128:(c + 1) * 128], dsl, reads=[dslb], start=True, stop=True)
            self.act(AR[:, yA + c, :], p2[:, :], AF.Identity, reads=[p2b], writes=[ARb[yA + c]],
                     scale=self.vcol(L, "pool_scale", c))

        tw = [self.AR32[:, 12:14, :].rearrange("p a b -> p (a b)"), self.AR32[:, 14:16, :].rearrange("p a b -> p (a b)")]
        twb = [ARb[24:28], ARb[28:32]]
        def pool_chunk(c):
            w = 2 << c
            src, srcb, off = self.Aw[:, c, :], [self.Awb[c]], 16
            lvl, k = 1, 0
            while lvl < w:
                halo = w - 2 * lvl
                n = T + halo
                dst, dstb = tw[k % 2], twb[k % 2]
                a0 = off - halo
                self.tt("dve", dst[:, 0:n], src[:, a0:a0 + n], src[:, a0 - lvl:a0 - lvl + n], ALU.add,
                        reads=srcb, writes=dstb)
                src, srcb, off = dst, dstb, halo
                lvl *= 2
                k += 1
            S_, Sb = src, srcb
            dsl, dslb = AR[:, DD + c, :], ARb[DD + c]
            self.stt(dsl, S_[:, 0:T], 1.0 / w, ALU.mult, self.Aw[:, c, 16:16 + T], ALU.subtract,
                     reads=Sb + [self.Awb[c]], writes=[dslb])
            if tt == 0:
                tmp, tmpb = self.small[:, 64:80], self.smallb[4]
                self.tt("dve", tmp, S_[:, 0:16], self.cnt[:, c, :], ALU.mult, reads=Sb, writes=[tmpb])
                self.tt("dve", dsl[:, 0:16], tmp, self.Aw[:, c, 16:32], ALU.subtract, reads=[tmpb, self.Awb[c]], writes=[dslb])
        def c_stats(c):
            ub, us = 36 + (c % 2), 38 + (c % 2)
            self.mm(s1[:, :], s1b, self.onesb[:, :], AR[:, ub, :], reads=[ARb[ub]], start=(c == 0), stop=(c == 3), sig=True)
            self.mm(s2[:, :], s2b, self.onesb[:, :], AR[:, us, :], reads=[ARb[us]], start=(c == 0), stop=(c == 3), sig=True)

        for c in range(4):
            py, pyb = self.bank()
            for k in range(31):
                d, db = self.mk_diag(self.vcol(L, "cconv", k * 4 + c))
                self.mm(py[:, :], pyb, d, self.Cw[:, c, k:k + T], reads=[db, self.Cwb[c]], start=(k == 0), stop=(k == 30), sig=True)
            yc, ycb = self._u32(YC32 + 2 * c), ARb[YC32 + 2 * c:YC32 + 2 * c + 2]
            ub, us = 36 + (c % 2), 38 + (c % 2)
            self.act(AR[:, ub, :], py[:, :], AF.Copy, reads=[pyb], writes=[ARb[ub]])
            self.act(AR[:, us, :], py[:, :], AF.Square, reads=[pyb], writes=[ARb[us]])
            self.act(yc, py[:, :], AF.Copy, reads=[pyb], writes=ycb)
            if c >= 1:
                c_stats(c - 1)
            pool_chunk(c)
        sBx, bBx = self.W(f"L{L}_win_Bxin")
        sBc, bBc = self.W(f"L{L}_win_Bcg")
        for c in range(4):
            px, pxb = self.proj(sBx, bBx, c)
            u = 28 + 2 * (c % 2)
            xs, xsb = self._u32(u), ARb[u:u + 2]
            self.act(xs, px[:, :], AF.Copy, reads=[pxb], writes=xsb)
            pc, pcb = self.proj(sBc, bBc, c)
            self.tt("dve", self.Bw[:, c, 2:2 + T], pc[:, :], xs, ALU.mult, reads=[pcb] + xsb, writes=[self.Bwb[c]])
            P.op("pool", lambda e, a=self.hB[:, L, c, :], b=self.Bw[:, c, T:T + 2]: e.tensor_copy(out=a, in_=b),
                 reads=[self.Bwb[c]], writes=[self.hBb[L][c]])
        self.release(2)
        sBg, bBg = self.W(f"L{L}_win_Bbg")
        for c in range(4):
            pg, pgb = self.proj(sBg, bBg, c)
            self.act(AR[:, BG + c, :], pg[:, :], AF.Copy, reads=[pgb], writes=[ARb[BG + c]])
        self.release()

        c_stats(3)
        sDu, bDu = self.W(f"L{L}_win_Du")
        for c in range(4):
            pu, pub = self.proj(sDu, bDu, c)
            self.act(AR[:, UU + c, :], pu[:, :], AF.Gelu_apprx_tanh, reads=[pub], writes=[ARb[UU + c]])
        self.release()
        for c in range(4):
            a_lin(c)
        mean, meanb = self._u32(MEAN), ARb[MEAN:MEAN + 2]
        var, varb = self._u32(VAR), ARb[VAR:VAR + 2]
        mr, mrb = self._u32(MR), ARb[MR:MR + 2]
        self.ts("dve", mean, s1[:, :], 1.0 / 512, ALU.mult, reads=[s1b], writes=meanb)
        self.tt("dve", mr, mean, mean, ALU.mult, reads=meanb, writes=mrb)
        self.stt(var, s2[:, :], 1.0 / 512, ALU.mult, mr, ALU.subtract, reads=[s2b] + mrb, writes=varb)
        self.act(var, var, AF.Sqrt, reads=varb, writes=varb, bias=EPS)
        for c in range(4):
            u = 28 + 2 * (c % 2)
            cv, cvb = self._u32(u), ARb[u:u + 2]
            self.ts("dve", cv, self.Bw[:, c, 0:T], self.vcol(L, "sconv", 0 * 4 + c), ALU.mult, reads=[self.Bwb[c]], writes=cvb)
            for k in (1, 2):
                self.stt(cv, self.Bw[:, c, k:k + T], self.vcol(L, "sconv", k * 4 + c), ALU.mult, cv, ALU.add,
                         reads=[self.Bwb[c]] + cvb, writes=cvb)
            self.tt("dve", AR[:, yB + c, :], cv, AR[:, BG + c, :], ALU.mult, reads=cvb + [ARb[BG + c]], writes=[ARb[yB + c]])
        P.op("dve", lambda e: e.reciprocal(out=var, in_=var), reads=varb, writes=varb)
        self.tt("dve", mr, mean, var, ALU.mult, reads=meanb + varb, writes=mrb)
        for hh in range(4):
            if hh < 2:
                pz, pzb = self.ps[6 + hh], self.psb[6 + hh]
            else:
                pz, pzb = self.bank()
            for nb in range(4):
                sl = pz[:, nb * 128:(nb + 1) * 128]
                self.mm(sl, pzb, AR[:, VLN + nb, hh * 128:(hh + 1) * 128], self.sguw[:, L, hh * 128:(hh + 1) * 128],
                        reads=[ARb[VLN + nb]], start=True, stop=False)
                bo = L * 512 + hh * 128
                self.mm(sl, pzb, self.onesb[0:1, :], self.bhl[0:1, 0, bo:bo + 128], reads=[], start=False, stop=False)
                self.mm(sl, pzb, self.onesb[0:1, :], self.bhl[0:1, 1, bo:bo + 128], reads=[], start=False, stop=True,
                        sig=(nb == 3))
            self.tt("dve", AR[:, yD + hh, :], pz[:, :], AR[:, UU + hh, :], ALU.mult, reads=[pzb, ARb[UU + hh]], writes=[ARb[yD + hh]])

        for c in range(4):
            yc, ycb = self._u32(YC32 + 2 * c), ARb[YC32 + 2 * c:YC32 + 2 * c + 2]
            self.tt("dve", yc, yc, var, ALU.mult, reads=ycb + varb, writes=ycb)
            self.tt("dve", yc, yc, mr, ALU.subtract, reads=ycb + mrb, writes=ycb)
            self.act(AR[:, yC + c, :], yc, AF.Silu, reads=ycb, writes=[ARb[yC + c]],
                     scale=self.vcol(L, "cln_g", c), bias=self.vcol(L, "cln_b", c))

        def merge_sum(m):
            t0 = TG + 4 * (m % 2)
            acc, accb = self._u32(34), ARb[34:36]
            self.tt("pool", acc, AR[:, t0, :], AR[:, t0 + 1, :], ALU.add, reads=[ARb[t0], ARb[t0 + 1]], writes=accb)
            self.tt("pool", acc, acc, AR[:, t0 + 2, :], ALU.add, reads=accb + [ARb[t0 + 2]], writes=accb)
            self.tt("pool", AR[:, MERG + m, :], acc, AR[:, t0 + 3, :], ALU.add, reads=accb + [ARb[t0 + 3]], writes=[ARb[MERG + m]])

        for m in range(8):
            sU, bU = self.W(f"L{L}_up_{m}")
            sG, bG = self.W(f"L{L}_gt_{m}")
            for g in (0, 1, 3, 2):
                if "ABCD"[g] not in BRANCHES:
                    continue
                ybase = (yA, yB, yC, yD)[g]
                pg, pgb = self.bank()
                for kc in range(KC):
                    o = g * 1024 + kc * 128
                    self.mm(pg[:, :], pgb, sG[:, o:o + 128], self.H[:, kc, :], reads=[bG, self.Hb[kc]],
                            start=(kc == 0), stop=(kc == KC - 1))
                u = 24 + 2 * (g % 2)
                gt, gtb = self._u32(u), ARb[u:u + 2]
                self.act(gt, pg[:, :], AF.Sigmoid, reads=[pgb], writes=gtb)
                pu, pub = self.bank()
                for k in range(4):
                    o = g * 512 + k * 128
                    self.mm(pu[:, :], pub, sU[:, o:o + 128], AR[:, ybase + k, :], reads=[bU, ARb[ybase + k]],
                            start=(k == 0), stop=(k == 3))
                tg = TG + g + 4 * (m % 2)
                self.tt("dve", AR[:, tg, :], pu[:, :], gt, ALU.mult, reads=[pub] + gtb, writes=[ARb[tg]])
            self.release(2)
            if m >= 1:
                merge_sum(m - 1)
        merge_sum(7)
        for hf in range(2):
            sO, bO = self.W(f"L{L}_wo_{hf}")
            for mm_ in range(4):
                m = hf * 4 + mm_
                po, pob = self.bank()
                for kc in range(KC):
                    o = kc * 512 + mm_ * 128
                    self.mm(po[:, :], pob, sO[:, o:o + 128], AR[:, MERG + kc, :], reads=[bO, ARb[MERG + kc]],
                            start=(kc == 0), stop=(kc == KC - 1))
                self.tt("dve", self.X[:, m, :], po[:, :], self.X[:, m, :], ALU.add, reads=[pob, self.Xb[m]], writes=[self.Xb[m]])
                if m >= 1:
                    self.xstat_feed(m - 1)
            self.release()
        self.xstat_feed(KC - 1)


def run(inputs, NT=S // T, NL=2, ncores=8, stages=("f1", "mix", "f2"), fin=True, trace=False):
    inp = {k: np.asarray(v, dtype=np.float32) for k, v in inputs.items()}
    host, offs = build_host_arrays(inp, NL)
    wtot = host["wall"].shape[1]
    b = Builder(NT, NL, offs, wtot, stages=stages, fin=fin)
    nc = b.build()
    SS = NT * T
    in_maps = []
    for i in range(ncores):
        m = dict(host)
        m["xT"] = np.ascontiguousarray(inp["x"][i, :SS, :].T)
        in_maps.append(m)
    res = run_bass_kernel_spmd(nc, in_maps, core_ids=list(range(ncores)), trace=trace)
    out = np.stack([np.ascontiguousarray(res.results[i]["oT"].T) for i in range(ncores)], 0)
    return out.astype(np.float32), res, b


def kernel(**inputs):
    out, _, _ = run(inputs)
    return out
```

```python
import numpy as np
from contextlib import ExitStack
import concourse.bass as bass
import concourse.mybir as mybir
from concourse.bass_utils import run_bass_kernel_spmd

F32 = mybir.dt.float32
BF16 = mybir.dt.bfloat16
AF = mybir.ActivationFunctionType
ALU = mybir.AluOpType

D = 1024
S = 4096
DFF = 2816
KC = 8
FC = 22
T = 512
EPS = 1e-6
NSLOT = 5
SLOT_E = 4096
STG_E = 2048
NSTG = 3
DEFER_FRAC = 0.4
SAME_ENG_SYNC = True
BRANCHES = "ABCD"


def _w_in_panel(w_in, c0):
    return w_in[:, c0:c0 + 512].reshape(KC, 128, 512).transpose(1, 0, 2).reshape(128, -1)


def layer_panels(L, inp):
    out = []

    def ffn(tag, w13, w2):
        for g in range(11):
            a = np.stack([w13[:, ab * DFF + g * 256: ab * DFF + (g + 1) * 256] for ab in range(2)], 0)
            a = a.reshape(2, KC, 128, 256).transpose(2, 0, 1, 3).reshape(128, -1)
            out.append((f"{tag}_w13_{g}", a))
        for m in range(8):
            a = w2[:, m * 128:(m + 1) * 128].reshape(FC, 128, 128).transpose(1, 0, 2).reshape(128, -1)
            out.append((f"{tag}_w2_{m}", a))

    ffn("f1", inp["ffn1_w13"][L], inp["ffn1_w2"][L])
    w_in = inp["w_in"][L]
    for nm, c0 in (("Cb", 2560), ("Ca", 2048), ("Bxin", 512), ("Bcg", 1536), ("Bbg", 1024),
                   ("A", 0), ("Du", 3072), ("Dv", 3584)):
        out.append((f"win_{nm}", _w_in_panel(w_in, c0)))
    w_up = inp["w_up"][L]
    for m in range(8):
        a = w_up[:, :, m * 128:(m + 1) * 128].reshape(4, 4, 128, 128).transpose(2, 0, 1, 3).reshape(128, -1)
        out.append((f"up_{m}", a))
        g = np.stack([w_in[:, 4096 + gi * 1024 + m * 128: 4096 + gi * 1024 + (m + 1) * 128] for gi in range(4)], 0)
        g = g.reshape(4, KC, 128, 128).transpose(2, 0, 1, 3).reshape(128, -1)
        out.append((f"gt_{m}", g))
    w_out = inp["w_out"][L]
    for hf in range(2):
        out.append((f"wo_{hf}", _w_in_panel(w_out, hf * 512)))
    ffn("f2", inp["ffn2_w13"][L], inp["ffn2_w2"][L])
    return out


VEC_LAYOUT = [("ffn1_norm", 8), ("mix_norm", 8), ("ffn2_norm", 8), ("pool_scale", 4), ("sconv", 12),
              ("cconv", 124), ("cln_g", 4), ("cln_b", 4)]
VEC_PER_LAYER = sum(n for _, n in VEC_LAYOUT)


def build_host_arrays(inp, NL):
    f = np.float32
    panels = []
    for L in range(NL):
        panels += [(f"L{L}_{n}", a) for n, a in layer_panels(L, inp)]
    offs = {}
    o = 0
    for n, a in panels:
        offs[n] = (o, a.shape[1])
        o += a.shape[1]
    wall = np.ascontiguousarray(np.concatenate([a for _, a in panels], axis=1).astype(f))
    NV = NL * VEC_PER_LAYER + 8
    vecs = np.zeros((128, NV), f)

    def chunked(v):
        return v.reshape(-1, 128).T

    for L in range(NL):
        base = L * VEC_PER_LAYER
        c = base
        for nm, n in VEC_LAYOUT:
            if nm in ("ffn1_norm", "mix_norm", "ffn2_norm", "pool_scale"):
                v = chunked(inp[nm][L])
            elif nm == "sconv":
                v = inp["sconv_w"][L].reshape(3, 4, 128).transpose(2, 0, 1).reshape(128, 12)
            elif nm == "cconv":
                v = inp["cconv_w"][L].reshape(31, 4, 128).transpose(2, 0, 1).reshape(128, 124)
            elif nm == "cln_g":
                v = chunked(inp["cconv_ln_g"][L])
            elif nm == "cln_b":
                v = chunked(inp["cconv_ln_b"][L])
            vecs[:, c:c + n] = v
            c += n
    vecs[:, NL * VEC_PER_LAYER:] = chunked(inp["final_norm"])
    bc = np.zeros((128, NL, 2, 512), f)
    wsm = np.zeros((128, NL, 2, 512), f)
    sgub = np.zeros((1, NL * 512), f)
    for L in range(NL):
        bc[:, L, 0, :] = inp["sgu_ln_g"][L][None, :]
        bc[:, L, 1, :] = inp["sgu_ln_b"][L][None, :]
        wsm[:, L, 0, :] = inp["pool_w"][L].transpose(1, 0, 2).reshape(128, 512)
        wsm[:, L, 1, :] = inp["sgu_w"][L].transpose(2, 0, 1).reshape(128, 512)
        sgub[0, L * 512:(L + 1) * 512] = inp["sgu_b"][L].reshape(512)
    cnt = np.zeros((128, 4, 16), f)
    for gi, w in enumerate((2, 4, 8, 16)):
        cnt[:, gi, :] = (1.0 / np.minimum(np.arange(16) + 1, w))[None, :]
    ident = np.eye(128, dtype=f)
    return dict(wall=wall, vecs=vecs, bc=bc, wsm=wsm, sgub=sgub, cnt=cnt, ident=ident), offs


class Tok:
    __slots__ = ("sem", "val", "eng")

    def __init__(self, sem, val, eng):
        self.sem, self.val, self.eng = sem, val, eng


class Buf:
    __slots__ = ("name", "w", "r")

    def __init__(self, name):
        self.name, self.w, self.r = name, None, {}


class Eng:
    def __init__(self, name, sem):
        self.name, self.sem = name, sem
        self.count = 0
        self.known = {}
        self.q = []
        self.pending = []


class DmaSem:
    def __init__(self, sem):
        self.sem, self.count = sem, 0


class _ArenaView:
    def __init__(self, bf):
        self.bf = bf

    def __getitem__(self, idx):
        p, u, f = idx
        assert isinstance(u, int)
        lo = 0 if f.start is None else f.start
        hi = 512 if f.stop is None else f.stop
        base = (u % 2) * 512
        return self.bf[p, u // 2, base + lo:base + hi]


class Prog:
    def __init__(self, nc, es):
        self.nc, self.es = nc, es
        self.eng = {}
        for n in ("pe", "act", "dve", "pool", "sp"):
            self.eng[n] = Eng(n, es.enter_context(nc.semaphore("sem_" + n)))
        self.n_wait = 0
        self.n_ins = 0

    def dsem(self, name):
        return DmaSem(self.es.enter_context(self.nc.semaphore(name)))

    def _waits(self, E, needs):
        best = {}
        for t in needs:
            if t is None:
                continue
            if t.eng is E and (E.name == "pe" or not SAME_ENG_SYNC):
                continue
            assert t.val is not None, "dependency on an unresolved (unsignalled) token"
            k = id(t.sem)
            if k not in best or best[k].val < t.val:
                best[k] = t
        for k, t in best.items():
            if E.known.get(k, 0) >= t.val:
                continue
            E.known[k] = t.val
            E.q.append(lambda e, s=t.sem, v=t.val: e.wait_ge(s, v))
            self.n_wait += 1

    def _needs(self, reads, writes):
        needs = []
        for b in reads:
            needs.append(b.w)
        for b in writes:
            needs.append(b.w)
            needs.extend(b.r.values())
        return needs

    def _commit(self, tok, key, reads, writes):
        for b in reads:
            b.r[key] = tok
        for b in writes:
            b.w = tok
            b.r = {}

    def op(self, en, fn, reads=(), writes=(), signal=True):
        E = self.eng[en]
        self._waits(E, self._needs(reads, writes))
        self.n_ins += 1
        if signal:
            E.count += 1
            tok = Tok(E.sem, E.count, E)
            for p in E.pending:
                p.val = E.count
            E.pending = []
            E.q.append(lambda e, fn=fn, s=E.sem: fn(e).then_inc(s, 1))
        else:
            tok = Tok(E.sem, None, E)
            E.pending.append(tok)
            E.q.append(lambda e, fn=fn: fn(e))
        self._commit(tok, id(E.sem), reads, writes)
        return tok

    def dma(self, qn, ds, pairs, reads=(), writes=()):
        E = self.eng[qn]
        self._waits(E, self._needs(reads, writes))
        for (o, i) in pairs:
            ds.count += 16
            E.q.append(lambda e, o=o, i=i, s=ds.sem: e.dma_start(out=o, in_=i).then_inc(s, 16))
            self.n_ins += 1
        tok = Tok(ds.sem, ds.count, None)
        self._commit(tok, id(ds.sem), reads, writes)
        return tok

    def wait_all(self, en, toks):
        self._waits(self.eng[en], toks)

    def replay(self, block):
        def mk(name):
            q = self.eng[name].q

            def run(e):
                for f in q:
                    f(e)
            return run
        block.tensor(mk("pe"))
        block.scalar(mk("act"))
        block.vector(mk("dve"))
        block.gpsimd(mk("pool"))
        block.sync(mk("sp"))


class Builder:
    def __init__(self, NT, NL, offs, wtot, stages=("f1", "mix", "f2"), fin=True):
        self.NT, self.NL, self.offs, self.wtot = NT, NL, offs, wtot
        self.stages, self.fin = stages, fin
        nc = self.nc = bass.Bass("TRN2", target_bir_lowering=False)
        NV = NL * VEC_PER_LAYER + 8
        self.NV = NV
        SS = NT * T
        self.d_x = nc.dram_tensor("xT", [D, SS], F32, kind="ExternalInput").ap()
        self.d_o = nc.dram_tensor("oT", [D, SS], F32, kind="ExternalOutput").ap()
        self.d_wall = nc.dram_tensor("wall", [128, wtot], F32, kind="ExternalInput").ap()
        self.d_vecs = nc.dram_tensor("vecs", [128, NV], F32, kind="ExternalInput").ap()
        self.d_bc = nc.dram_tensor("bc", [128, NL, 2, 512], F32, kind="ExternalInput").ap()
        self.d_wsm = nc.dram_tensor("wsm", [128, NL, 2, 512], F32, kind="ExternalInput").ap()
        self.d_sgub = nc.dram_tensor("sgub", [1, NL * 512], F32, kind="ExternalInput").ap()
        self.d_cnt = nc.dram_tensor("cnt", [128, 4, 16], F32, kind="ExternalInput").ap()
        self.d_ident = nc.dram_tensor("ident", [128, 128], F32, kind="ExternalInput").ap()
        self.d_wbf = nc.dram_tensor("wbf", [128, wtot], BF16, kind="Internal").ap()

    def sb(self, name, shape, dt):
        return self.es.enter_context(self.nc.sbuf_tensor(name, shape, dt))

    def build(self):
        nc = self.nc
        NL = self.NL
        with ExitStack() as es:
            self.es = es
            P = self.P = Prog(nc, es)
            self.X = self.sb("X", [128, KC, T], F32)
            self.Xb = [Buf(f"X{k}") for k in range(KC)]
            self.H = self.sb("H", [128, KC, T], BF16)
            self.Hb = [Buf(f"H{k}") for k in range(KC)]
            NU = 80
            self.AR32 = self.sb("AR", [128, NU // 2, T], F32)
            self.ARbf = self.AR32[:, :, :].bitcast(BF16)
            self.AR = _ArenaView(self.ARbf)
            self.ARb = [Buf(f"U{k}") for k in range(NU)]
            self.ring = [self.sb(f"ring{i}", [128, SLOT_E], BF16) for i in range(NSLOT)]
            self.ringb = [Buf(f"ring{i}") for i in range(NSLOT)]
            self.ring_ld = [P.dsem(f"ringld{i}") for i in range(NSLOT)]
            self.ring_st = [P.dsem(f"ringst{i}") for i in range(NSLOT)]
            self.stg = [self.sb(f"stg{i}", [128, STG_E], F32) for i in range(NSTG)]
            self.stgb = [Buf(f"stg{i}") for i in range(NSTG)]
            self.stg_ld = [P.dsem(f"stgld{i}") for i in range(NSTG)]
            self.Aw = self.sb("Aw", [128, 4, 16 + T], F32)
            self.Bw = self.sb("Bw", [128, 4, 2 + T], BF16)
            self.Cw = self.sb("Cw", [128, 4, 30 + T], BF16)
            self.Awb = [Buf(f"Aw{c}") for c in range(4)]
            self.Bwb = [Buf(f"Bw{c}") for c in range(4)]
            self.Cwb = [Buf(f"Cw{c}") for c in range(4)]
            self.hA = self.sb("hA", [128, NL, 4, 16], F32)
            self.hB = self.sb("hB", [128, NL, 4, 2], BF16)
            self.hC = self.sb("hC", [128, NL, 4, 30], BF16)
            self.hAb = [[Buf(f"hA{l}{c}") for c in range(4)] for l in range(NL)]
            self.hBb = [[Buf(f"hB{l}{c}") for c in range(4)] for l in range(NL)]
            self.hCb = [[Buf(f"hC{l}{c}") for c in range(4)] for l in range(NL)]
            self.vecs = self.sb("vecs_sb", [128, self.NV], F32)
            self.bc = self.sb("bc_sb", [128, NL, 2, 512], F32)
            self.poolw = self.sb("poolw", [128, NL, 512], BF16)
            self.sguw = self.sb("sguw", [128, NL, 512], BF16)
            self.bhl = self.sb("bhl", [1, 2, NL * 512], BF16)
            self.cnt = self.sb("cnt_sb", [128, 4, 16], F32)
            self.id32 = self.sb("id32", [128, 128], F32)
            self.idb = self.sb("idb", [128, 128], BF16)
            self.ones32 = self.sb("ones32", [128, 128], F32)
            self.onesb = self.sb("onesb", [128, 128], BF16)
            ND = 8
            self.diag = self.sb("diag", [128, ND, 128], BF16)
            self.diagb = [Buf(f"diag{i}") for i in range(ND)]
            self.diag_i = 0
            self.mhalf = self.sb("mhalf", [128, 2], F32)
            self.small = self.sb("small", [128, 96], F32)
            self.smallb = [Buf(f"small{i}") for i in range(5)]
            self.constb = Buf("consts")
            self.wbfb = {}
            self.ps = [es.enter_context(nc.psum_tensor(f"ps{i}", [128, T], F32)) for i in range(8)]
            self.psb = [Buf(f"ps{i}") for i in range(8)]
            self.ps_i = 0
            self.ds_x = [P.dsem(f"ds_x{k}") for k in range(KC)]
            self.ds_o = [P.dsem(f"ds_o{k}") for k in range(KC)]
            self.ds_c = P.dsem("ds_c")
            self.ds_dbg = P.dsem("ds_dbg")
            self.plan = []
            for L in range(NL):
                self.plan += [f"L{L}_{n}" for n in self.layer_order()]
            self.fetched = 0
            self.used = 0
            self.released = 0
            self.deferred = []
            self.cast_flip = 0

            self.emit_all()
            block = es.enter_context(nc.Block())
            P.replay(block)
        return nc

    def layer_order(self):
        o = []
        if "f1" in self.stages:
            o += [f"f1_w13_{g}" for g in range(11)] + [f"f1_w2_{m}" for m in range(8)]
        if "mix" in self.stages:
            o += ["win_Cb", "win_Ca", "win_Dv", "win_A", "win_Bbg", "win_Bxin", "win_Bcg", "win_Du"]
            for m in range(8):
                o += [f"up_{m}", f"gt_{m}"]
            o += ["wo_0", "wo_1"]
        if "f2" in self.stages:
            o += [f"f2_w13_{g}" for g in range(11)] + [f"f2_w2_{m}" for m in range(8)]
        return o

    def bank(self):
        i = self.ps_i
        self.ps_i = (i + 1) % 6
        return self.ps[i], self.psb[i]

    def vcol(self, L, name, j=0):
        c = L * VEC_PER_LAYER
        for nm, n in VEC_LAYOUT:
            if nm == name:
                return self.vecs[:, c + j:c + j + 1]
            c += n
        raise KeyError(name)

    def _emit_fetch(self, gi):
        P = self.P
        name = self.plan[gi % len(self.plan)]
        tile_i = gi // len(self.plan)
        off, n = self.offs[name]
        s = gi % NSLOT
        slot, sbuf = self.ring[s], self.ringb[s]
        if name not in self.wbfb:
            self.wbfb[name] = Buf("wbf_" + name)
        wb = self.wbfb[name]
        pi = gi % len(self.plan)
        deferp = self.NT >= 2 and pi >= int(len(self.plan) * (1.0 - DEFER_FRAC))
        fp32_path = (tile_i == 0) or (tile_i == 1 and deferp)
        do_store = (tile_i == 0 and not deferp) or (tile_i == 1 and deferp) or self.NT < 2
        if fp32_path:
            while len(self.deferred) > (1 if do_store else 0):
                self._flush_store()
            o = 0
            while o < n:
                ln = min(STG_E, n - o)
                h = self.cast_flip % NSTG
                ce = "dve" if self.cast_flip % 2 == 0 else "act"
                self.cast_flip += 1
                P.dma("sp", self.stg_ld[h], [(self.stg[h][:, 0:ln], self.d_wall[:, off + o:off + o + ln])],
                      writes=[self.stgb[h]])
                if ce == "dve":
                    P.op("dve", lambda e, a=slot[:, o:o + ln], b=self.stg[h][:, 0:ln]: e.tensor_copy(out=a, in_=b),
                         reads=[self.stgb[h]], writes=[sbuf])
                else:
                    P.op("act", lambda e, a=slot[:, o:o + ln], b=self.stg[h][:, 0:ln]: e.activation(out=a, in_=b, func=AF.Copy),
                         reads=[self.stgb[h]], writes=[sbuf])
                o += ln
            if do_store:
                self.deferred.append((s, off, n, wb))
        else:
            while self.deferred:
                self._flush_store()
            P.dma("sp", self.ring_ld[s], [(slot[:, 0:n], self.d_wbf[:, off:off + n])], reads=[wb], writes=[sbuf])

    def _flush_store(self):
        s, off, n, wb = self.deferred.pop(0)
        self.P.dma("sp", self.ring_st[s], [(self.d_wbf[:, off:off + n], self.ring[s][:, 0:n])],
                   reads=[self.ringb[s]], writes=[wb])

    def W(self, name):
        gi = self.used
        assert self.plan[gi % len(self.plan)] == name, (self.plan[gi % len(self.plan)], name)
        while self.fetched <= gi:
            assert self.fetched - NSLOT < self.released, "ring too small for live panels"
            self._emit_fetch(self.fetched)
            self.fetched += 1
        self.used += 1
        s = gi % NSLOT
        return self.ring[s], self.ringb[s]

    def release(self, n=1):
        self.released += n
        assert self.released <= self.used
        self.prefetch()

    def prefetch(self):
        total = len(self.plan) * self.NT
        while self.fetched < min(total, self.released + NSLOT):
            self._emit_fetch(self.fetched)
            self.fetched += 1

    def mm(self, out, outb, lhsT, rhs, reads, start, stop, sig=None):
        self.P.op("pe", lambda e: e.matmul(out, lhsT=lhsT, rhs=rhs, start=start, stop=stop),
                  reads=reads, writes=[outb], signal=(stop if sig is None else sig))

    def act(self, out, in_, func, reads, writes, scale=1.0, bias=0.0):
        self.P.op("act", lambda e: e.activation(out=out, in_=in_, func=func, scale=scale, bias=bias),
                  reads=reads, writes=writes)

    def tt(self, en, out, in0, in1, op, reads, writes):
        self.P.op(en, lambda e: e.tensor_tensor(out=out, in0=in0, in1=in1, op=op), reads=reads, writes=writes)

    def stt(self, out, in0, scalar, op0, in1, op1, reads, writes):
        self.P.op("dve", lambda e: e.scalar_tensor_tensor(out=out, in0=in0, scalar=scalar, op0=op0, in1=in1, op1=op1),
                  reads=reads, writes=writes)

    def ts(self, en, out, in0, s1, op0, reads, writes, s2=None, op1=ALU.bypass):
        self.P.op(en, lambda e: e.tensor_scalar(out=out, in0=in0, scalar1=s1, op0=op0, scalar2=s2, op1=op1),
                  reads=reads, writes=writes)

    def _u32(self, u):
        return self.AR32[:, u // 2, :]

    def emit_all(self):
        P, nc = self.P, self.nc
        NL = self.NL
        cb = self.constb
        P.dma("sp", self.ds_c, [(self.vecs[:, :], self.d_vecs), (self.id32[:, :], self.d_ident)], writes=[cb])
        P.op("act", lambda e: e.activation(out=self.idb[:, :], in_=self.id32[:, :], func=AF.Copy), reads=[cb], writes=[cb])
        P.op("pool", lambda e: e.memset(self.onesb[:, :], 1.0), writes=[cb])
        P.op("pool", lambda e: e.memset(self.ones32[:, :], 1.0), writes=[cb])
        P.op("pool", lambda e: e.memset(self.mhalf[:, :], -0.5), writes=[cb])
        for en in ("pe", "act", "dve", "pool"):
            P.wait_all(en, [cb.w] + list(cb.r.values()))
        xv = self.d_x.rearrange("(k p) s -> p k s", p=128)
        for kc in range(KC):
            P.dma("sp", self.ds_x[kc], [(self.X[:, kc, :], xv[:, kc, 0:T])], writes=[self.Xb[kc]])
        self.prefetch()
        cb2 = self.constb2 = Buf("consts2")
        P.dma("sp", self.ds_c, [(self.bc[:, :, :, :], self.d_bc), (self.cnt[:, :, :], self.d_cnt)], writes=[cb2])
        for L in range(NL):
            for w in range(2):
                h = w
                P.dma("sp", self.stg_ld[h], [(self.stg[h][:, 0:512], self.d_wsm[:, L, w, :])], writes=[self.stgb[h]])
                dst = (self.poolw if w == 0 else self.sguw)[:, L, :]
                P.op("act", lambda e, a=dst, b=self.stg[h][:, 0:512]: e.activation(out=a, in_=b, func=AF.Copy),
                     reads=[self.stgb[h]], writes=[cb2])
            for hh in range(4):
                P.op("pool", lambda e, a=self.sguw[64:128, L, hh * 128:hh * 128 + 64]: e.memset(a, 0.0), writes=[cb2])
        nb_ = NL * 512
        b32 = self.stg[0][0:1, 0:nb_]
        hi32 = self.stg[0][0:1, nb_:2 * nb_]
        P.dma("sp", self.stg_ld[0], [(b32, self.d_sgub)], writes=[self.stgb[0]])
        P.op("act", lambda e: e.activation(out=self.bhl[0:1, 0, :], in_=b32, func=AF.Copy), reads=[self.stgb[0]], writes=[cb2])
        P.op("act", lambda e: e.activation(out=hi32, in_=self.bhl[0:1, 0, :], func=AF.Copy), reads=[cb2], writes=[self.stgb[0]])
        P.op("dve", lambda e: e.tensor_tensor(out=self.bhl[0:1, 1, :], in0=b32, in1=hi32, op=ALU.subtract),
             reads=[self.stgb[0]], writes=[cb2])
        for L in range(NL):
            P.op("pool", lambda e, a=self.hA[:, L, :, :]: e.memset(a, 0.0), writes=self.hAb[L])
            P.op("pool", lambda e, a=self.hB[:, L, :, :]: e.memset(a, 0.0), writes=self.hBb[L])
            P.op("pool", lambda e, a=self.hC[:, L, :, :]: e.memset(a, 0.0), writes=self.hCb[L])
        self.c2_waited = False

        ov = self.d_o.rearrange("(k p) s -> p k s", p=128)
        full = self.fin and tuple(self.stages) == ("f1", "mix", "f2")
        self.h_ready = False
        XP = 40

        def xp(kc):
            return self._u32(XP + 2 * kc), self.ARb[XP + 2 * kc:XP + 2 * kc + 2]

        def next_h():
            ps, psb = self.ps[7], self.psb[7]
            for kc in range(KC):
                u = 36 + (kc % 4)
                sq, sqb = self.AR[:, u, :], [self.ARb[u]]
                xa, xb = xp(kc)
                self.act(sq, xa, AF.Square, reads=xb, writes=sqb)
                self.mm(ps[:, :], psb, self.onesb[:, :], sq, reads=sqb, start=(kc == 0), stop=(kc == KC - 1), sig=True)
            self.sqrt_table_warm()
            r, rb = self._u32(34), self.ARb[34:36]
            self.act(r, ps[:, :], AF.Sqrt, reads=[psb], writes=rb, scale=1.0 / D, bias=EPS)
            P.op("dve", lambda e: e.reciprocal(out=r, in_=r), reads=rb, writes=rb)
            for kc in range(KC):
                xa, xb = xp(kc)
                self.stt(self.H[:, kc, :], xa, self.vcol(0, "ffn1_norm", kc), ALU.mult, r, ALU.mult,
                         reads=xb + rb, writes=[self.Hb[kc]])
            self.h_ready = True

        for tt in range(self.NT):
            if (tt > 0 and not full):
                for kc in range(KC):
                    P.dma("sp", self.ds_x[kc], [(self.X[:, kc, :], xv[:, kc, tt * T:(tt + 1) * T])], writes=[self.Xb[kc]])
            pre = full and tt + 1 < self.NT
            for L in range(NL):
                if "f1" in self.stages:
                    self.ffn(L, "f1", "ffn1_norm")
                if "mix" in self.stages:
                    self.mixer(L, tt)
                if "f2" in self.stages:
                    if pre and L == NL - 1:
                        for kc in range(KC):
                            xa, xb = xp(kc)
                            P.dma("sp", self.ds_x[kc], [(xa, xv[:, kc, (tt + 1) * T:(tt + 2) * T])], writes=xb)
                        self.ffn(L, "f2", "ffn2_norm", hook=next_h)
                    else:
                        self.ffn(L, "f2", "ffn2_norm")
            if self.fin:
                self.final_norm()
            for kc in range(KC):
                P.dma("sp", self.ds_o[kc], [(ov[:, kc, tt * T:(tt + 1) * T], self.X[:, kc, :])], reads=[self.Xb[kc]])
            if pre:
                for kc in range(KC):
                    xa, xb = xp(kc)
                    P.op("pool", lambda e, a=self.X[:, kc, :], b=xa: e.tensor_copy(out=a, in_=b), reads=xb, writes=[self.Xb[kc]])
        while self.deferred:
            self._flush_store()
        toks = [Tok(d.sem, d.count, None) for d in self.ds_o]
        for d in self.ring_st:
            if d.count:
                toks.append(Tok(d.sem, d.count, None))
        P.wait_all("sp", toks)

    def xstat_feed(self, kc):
        ps, psb = self.ps[6], self.psb[6]
        u = 36 + (kc % 4)
        sq, sqb = self.AR[:, u, :], [self.ARb[u]]
        self.act(sq, self.X[:, kc, :], AF.Square, reads=[self.Xb[kc]], writes=sqb)
        self.mm(ps[:, :], psb, self.onesb[:, :], sq, reads=sqb, start=(kc == 0), stop=(kc == KC - 1), sig=True)
        if kc == KC - 1:
            self.xs_ready = True
            self.sqrt_table_warm()

    def sqrt_table_warm(self):
        c = self.small[:, 88:89]
        self.P.op("act", lambda e: e.activation(out=c, in_=self.mhalf[:, 1:2], func=AF.Sqrt, scale=-1.0), writes=[self.smallb[4]])

    def rms_stats(self):
        ps, psb = self.ps[6], self.psb[6]
        if not getattr(self, "xs_ready", False):
            for kc in range(KC):
                self.xstat_feed(kc)
        self.xs_ready = False
        r, rb = self._u32(32), self.ARb[32:34]
        self.act(r, ps[:, :], AF.Sqrt, reads=[psb], writes=rb, scale=1.0 / D, bias=EPS)
        self.P.op("dve", lambda e: e.reciprocal(out=r, in_=r), reads=rb, writes=rb)
        return r, rb

    def rmsnorm_to_H(self, L, gname):
        r, rb = self.rms_stats()
        for kc in range(KC):
            self.stt(self.H[:, kc, :], self.X[:, kc, :], self.vcol(L, gname, kc), ALU.mult, r, ALU.mult,
                     reads=[self.Xb[kc]] + rb, writes=[self.Hb[kc]])

    def final_norm(self):
        r, rb = self.rms_stats()
        c0 = self.NL * VEC_PER_LAYER
        for kc in range(KC):
            self.stt(self.X[:, kc, :], self.X[:, kc, :], self.vecs[:, c0 + kc:c0 + kc + 1], ALU.mult, r, ALU.mult,
                     reads=[self.Xb[kc]] + rb, writes=[self.Xb[kc]])

    def ffn(self, L, tag, gname, hook=None):
        if self.h_ready:
            self.h_ready = False
        else:
            self.rmsnorm_to_H(L, gname)
        for g in range(11):
            slot, sbuf = self.W(f"L{L}_{tag}_w13_{g}")
            pre = None
            if g == 0:
                pre = [self.bank() for _ in range(4)]
                for kc in range(KC):
                    for idx, (p_, pb_) in enumerate(pre):
                        o = (idx % 2) * 2048 + kc * 256 + (idx // 2) * 128
                        self.mm(p_[:, :], pb_, slot[:, o:o + 128], self.H[:, kc, :], reads=[sbuf, self.Hb[kc]],
                                start=(kc == 0), stop=(kc == KC - 1))
            for jj in range(2):
                j = 2 * g + jj
                if pre is not None:
                    (pa, pab), (pb, pbb) = pre[2 * jj], pre[2 * jj + 1]
                else:
                    pa, pab = self.bank()
                    for kc in range(KC):
                        o = kc * 256 + jj * 128
                        self.mm(pa[:, :], pab, slot[:, o:o + 128], self.H[:, kc, :], reads=[sbuf, self.Hb[kc]],
                                start=(kc == 0), stop=(kc == KC - 1))
                    pb, pbb = self.bank()
                    for kc in range(KC):
                        o = 2048 + kc * 256 + jj * 128
                        self.mm(pb[:, :], pbb, slot[:, o:o + 128], self.H[:, kc, :], reads=[sbuf, self.Hb[kc]],
                                start=(kc == 0), stop=(kc == KC - 1))
                u = 24 + 2 * (j % 2)
                sa, sab = self._u32(u), self.ARb[u:u + 2]
                self.act(sa, pa[:, :], AF.Silu, reads=[pab], writes=sab)
                self.tt("dve", self.AR[:, j, :], pb[:, :], sa, ALU.mult, reads=[pbb] + sab, writes=[self.ARb[j]])
            self.release()
        for m in range(KC):
            slot, sbuf = self.W(f"L{L}_{tag}_w2_{m}")
            po, pob = self.bank()
            for j in range(FC):
                self.mm(po[:, :], pob, slot[:, j * 128:(j + 1) * 128], self.AR[:, j, :], reads=[sbuf, self.ARb[j]],
                        start=(j == 0), stop=(j == FC - 1))
            self.release()
            self.stt(self.X[:, m, :], po[:, :], 0.5, ALU.mult, self.X[:, m, :], ALU.add,
                     reads=[pob, self.Xb[m]], writes=[self.Xb[m]])
            if m >= 1:
                self.xstat_feed(m - 1)
        if hook is not None:
            hook()
        self.xstat_feed(KC - 1)

    def mk_diag(self, col_ap):
        i = self.diag_i
        self.diag_i = (i + 1) % len(self.diagb)
        d, db = self.diag[:, i, :], self.diagb[i]
        self.ts("dve", d, self.id32[:, :], col_ap, ALU.mult, reads=[], writes=[db])
        return d, db

    def proj(self, slot, sbuf, c):
        p, pb = self.bank()
        for kc in range(KC):
            o = kc * 512 + c * 128
            self.mm(p[:, :], pb, slot[:, o:o + 128], self.H[:, kc, :], reads=[sbuf, self.Hb[kc]],
                    start=(kc == 0), stop=(kc == KC - 1))
        return p, pb

    def mixer(self, L, tt):
        P = self.P
        AR, ARb = self.AR, self.ARb
        if not self.c2_waited:
            self.c2_waited = True
            cb2 = self.constb2
            for en in ("pe", "act", "dve", "pool"):
                P.wait_all(en, [cb2.w] + list(cb2.r.values()))
        self.rmsnorm_to_H(L, "mix_norm")
        yA, yB, yC, yD = 0, 4, 8, 12
        MERG = 16
        BG, YC32, UU, VLN, DD, TG = 46, 52, 60, 64, 72, 72
        MEAN, VAR, MR = 40, 42, 44
        s1, s1b = self.ps[6], self.psb[6]
        s2, s2b = self.ps[7], self.psb[7]

        for c in range(4):
            P.op("pool", lambda e, a=self.Cw[:, c, 0:30], b=self.hC[:, L, c, :]: e.tensor_copy(out=a, in_=b),
                 reads=[self.hCb[L][c]], writes=[self.Cwb[c]])
        for c in range(4):
            P.op("pool", lambda e, a=self.Bw[:, c, 0:2], b=self.hB[:, L, c, :]: e.tensor_copy(out=a, in_=b),
                 reads=[self.hBb[L][c]], writes=[self.Bwb[c]])
        for c in range(4):
            P.op("pool", lambda e, a=self.Aw[:, c, 0:16], b=self.hA[:, L, c, :]: e.tensor_copy(out=a, in_=b),
                 reads=[self.hAb[L][c]], writes=[self.Awb[c]])
        sCb, bCb = self.W(f"L{L}_win_Cb")
        sCa, bCa = self.W(f"L{L}_win_Ca")
        cbk = {}
        for c0 in (0, 2):
            for c in (c0, c0 + 1):
                cbk[("b", c)] = self.bank()
                cbk[("a", c)] = self.bank()
            for kc in range(KC):
                for c in (c0, c0 + 1):
                    for nm, slot_, sb_ in (("b", sCb, bCb), ("a", sCa, bCa)):
                        p_, pb2 = cbk[(nm, c)]
                        o = kc * 512 + c * 128
                        self.mm(p_[:, :], pb2, slot_[:, o:o + 128], self.H[:, kc, :], reads=[sb_, self.Hb[kc]],
                                start=(kc == 0), stop=(kc == KC - 1))
            for c in (c0, c0 + 1):
                pb_, pbb = cbk[("b", c)]
                u = 24 + 2 * (c % 2)
                sg, sgb = self._u32(u), ARb[u:u + 2]
                self.act(sg, pb_[:, :], AF.Sigmoid, reads=[pbb], writes=sgb)
                pa_, pab = cbk[("a", c)]
                self.tt("dve", self.Cw[:, c, 30:30 + T], pa_[:, :], sg, ALU.mult, reads=[pab] + sgb, writes=[self.Cwb[c]])
                P.op("pool", lambda e, a=self.hC[:, L, c, :], b=self.Cw[:, c, T:T + 30]: e.tensor_copy(out=a, in_=b),
                     reads=[self.Cwb[c]], writes=[self.hCb[L][c]])
        self.release(2)
        sDv, bDv = self.W(f"L{L}_win_Dv")
        for nb in range(4):
            pv, pvb = self.bank()
            for kc in range(KC):
                self.mm(pv[:, :], pvb, self.H[:, kc, nb * 128:(nb + 1) * 128], sDv[:, kc * 512:(kc + 1) * 512],
                        reads=[bDv, self.Hb[kc]], start=(kc == 0), stop=(kc == KC - 1))
            u = (68, 70, 50, 76)[nb]
            gv, gvb = self._u32(u), ARb[u:u + 2]
            self.act(gv, pv[:, :], AF.Gelu_apprx_tanh, reads=[pvb], writes=gvb)
            sm, smb = self.small[:, nb * 16:nb * 16 + 16], self.smallb[nb]
            P.op("dve", lambda e, o=sm[:, 0:6], i=gv: e.bn_stats(out=o, in_=i), reads=gvb, writes=[smb])
            P.op("dve", lambda e, o=sm[:, 6:8], i=sm[:, 0:6]: e.bn_aggr(out=o, in_=i), reads=[smb], writes=[smb])
            self.ts("pool", sm[:, 8:9], sm[:, 7:8], EPS, ALU.add, reads=[smb], writes=[smb])
            self.tt("pool", sm[:, 9:10], sm[:, 8:9], self.mhalf[:, 0:1], ALU.pow, reads=[smb], writes=[smb])
            self.tt("pool", sm[:, 10:11], sm[:, 6:7], sm[:, 9:10], ALU.mult, reads=[smb], writes=[smb])
            self.tt("pool", sm[:, 11:12], sm[:, 10:11], self.mhalf[:, 0:1], ALU.mult, reads=[smb], writes=[smb])
            self.tt("pool", sm[:, 12:13], sm[:, 11:12], sm[:, 11:12], ALU.add, reads=[smb], writes=[smb])

        def dv_finish():
            for nb in range(4):
                u = (68, 70, 50, 76)[nb]
                gv, gvb = self._u32(u), ARb[u:u + 2]
                sm, smb = self.small[:, nb * 16:nb * 16 + 16], self.smallb[nb]
                self.act(gv, gv, AF.Identity, reads=gvb + [smb], writes=gvb, scale=sm[:, 9:10], bias=sm[:, 12:13])
                self.tt("pool", gv, gv, self.bc[:, L, 0, :], ALU.mult, reads=gvb, writes=gvb)
                self.tt("pool", AR[:, VLN + nb, :], gv, self.bc[:, L, 1, :], ALU.add, reads=gvb, writes=[ARb[VLN + nb]])
        self.release()
        sA, bA = self.W(f"L{L}_win_A")
        for c in range(4):
            pa_, pab = self.proj(sA, bA, c)
            self.act(self.Aw[:, c, 16:16 + T], pa_[:, :], AF.Copy, reads=[pab], writes=[self.Awb[c]])
            P.op("pool", lambda e, a=self.hA[:, L, c, :], b=self.Aw[:, c, T:T + 16]: e.tensor_copy(out=a, in_=b),
                 reads=[self.Awb[c]], writes=[self.hAb[L][c]])
        self.release()
        dv_finish()
        def a_lin(c):
            dsl, dslb = AR[:, DD + c, :], ARb[DD + c]
            p2, p2b = self.bank()
            self.mm(p2[:, :], p2b, self.poolw[:, L, c * 128:(c + 1) * 128], dsl, reads=[dslb], start=True, stop=True)
            self.act(AR[:, yA + c, :], p2[:, :], AF.Identity, reads=[p2b], writes=[ARb[yA + c]],
                     scale=self.vcol(L, "pool_scale", c))

        tw = [self.AR32[:, 12:14, :].rearrange("p a b -> p (a b)"), self.AR32[:, 14:16, :].rearrange("p a b -> p (a b)")]
        twb = [ARb[24:28], ARb[28:32]]
        def pool_chunk(c):
            w = 2 << c
            src, srcb, off = self.Aw[:, c, :], [self.Awb[c]], 16
            lvl, k = 1, 0
            while lvl < w:
                halo = w - 2 * lvl
                n = T + halo
                dst, dstb = tw[k % 2], twb[k % 2]
                a0 = off - halo
                self.tt("dve", dst[:, 0:n], src[:, a0:a0 + n], src[:, a0 - lvl:a0 - lvl + n], ALU.add,
                        reads=srcb, writes=dstb)
                src, srcb, off = dst, dstb, halo
                lvl *= 2
                k += 1
            S_, Sb = src, srcb
            dsl, dslb = AR[:, DD + c, :], ARb[DD + c]
            self.stt(dsl, S_[:, 0:T], 1.0 / w, ALU.mult, self.Aw[:, c, 16:16 + T], ALU.subtract,
                     reads=Sb + [self.Awb[c]], writes=[dslb])
            if tt == 0:
                tmp, tmpb = self.small[:, 64:80], self.smallb[4]
                self.tt("dve", tmp, S_[:, 0:16], self.cnt[:, c, :], ALU.mult, reads=Sb, writes=[tmpb])
                self.tt("dve", dsl[:, 0:16], tmp, self.Aw[:, c, 16:32], ALU.subtract, reads=[tmpb, self.Awb[c]], writes=[dslb])
        def c_stats(c):
            ub, us = 36 + (c % 2), 38 + (c % 2)
            self.mm(s1[:, :], s1b, self.onesb[:, :], AR[:, ub, :], reads=[ARb[ub]], start=(c == 0), stop=(c == 3), sig=True)
            self.mm(s2[:, :], s2b, self.onesb[:, :], AR[:, us, :], reads=[ARb[us]], start=(c == 0), stop=(c == 3), sig=True)

        for c in range(4):
            py, pyb = self.bank()
            for k in range(31):
                d, db = self.mk_diag(self.vcol(L, "cconv", k * 4 + c))
                self.mm(py[:, :], pyb, d, self.Cw[:, c, k:k + T], reads=[db, self.Cwb[c]], start=(k == 0), stop=(k == 30), sig=True)
            yc, ycb = self._u32(YC32 + 2 * c), ARb[YC32 + 2 * c:YC32 + 2 * c + 2]
            ub, us = 36 + (c % 2), 38 + (c % 2)
            self.act(AR[:, ub, :], py[:, :], AF.Copy, reads=[pyb], writes=[ARb[ub]])
            self.act(AR[:, us, :], py[:, :], AF.Square, reads=[pyb], writes=[ARb[us]])
            self.act(yc, py[:, :], AF.Copy, reads=[pyb], writes=ycb)
            if c >= 1:
                c_stats(c - 1)
            pool_chunk(c)
        sBg, bBg = self.W(f"L{L}_win_Bbg")
        for c in range(4):
            pg, pgb = self.proj(sBg, bBg, c)
            self.act(AR[:, BG + c, :], pg[:, :], AF.Copy, reads=[pgb], writes=[ARb[BG + c]])
        self.release()

        def conv3(c):
            u = 28 + 2 * (c % 2)
            cv, cvb = self._u32(u), ARb[u:u + 2]
            self.ts("dve", cv, self.Bw[:, c, 0:T], self.vcol(L, "sconv", 0 * 4 + c), ALU.mult, reads=[self.Bwb[c]], writes=cvb)
            for k in (1, 2):
                self.stt(cv, self.Bw[:, c, k:k + T], self.vcol(L, "sconv", k * 4 + c), ALU.mult, cv, ALU.add,
                         reads=[self.Bwb[c]] + cvb, writes=cvb)
            self.tt("dve", AR[:, yB + c, :], cv, AR[:, BG + c, :], ALU.mult, reads=cvb + [ARb[BG + c]], writes=[ARb[yB + c]])

        sBx, bBx = self.W(f"L{L}_win_Bxin")
        sBc, bBc = self.W(f"L{L}_win_Bcg")
        for c in range(4):
            px, pxb = self.proj(sBx, bBx, c)
            u = 28 + 2 * (c % 2)
            xs, xsb = self._u32(u), ARb[u:u + 2]
            self.act(xs, px[:, :], AF.Copy, reads=[pxb], writes=xsb)
            pc, pcb = self.proj(sBc, bBc, c)
            self.tt("dve", self.Bw[:, c, 2:2 + T], pc[:, :], xs, ALU.mult, reads=[pcb] + xsb, writes=[self.Bwb[c]])
            P.op("pool", lambda e, a=self.hB[:, L, c, :], b=self.Bw[:, c, T:T + 2]: e.tensor_copy(out=a, in_=b),
                 reads=[self.Bwb[c]], writes=[self.hBb[L][c]])
            if c >= 1:
                conv3(c - 1)
        self.release(2)
        conv3(3)
        c_stats(3)
        sDu, bDu = self.W(f"L{L}_win_Du")
        for c in range(4):
            pu, pub = self.proj(sDu, bDu, c)
            self.act(AR[:, UU + c, :], pu[:, :], AF.Gelu_apprx_tanh, reads=[pub], writes=[ARb[UU + c]])
        self.release()
        for c in range(4):
            a_lin(c)
        mean, meanb = self._u32(MEAN), ARb[MEAN:MEAN + 2]
        var, varb = self._u32(VAR), ARb[VAR:VAR + 2]
        mr, mrb = self._u32(MR), ARb[MR:MR + 2]
        self.ts("dve", mean, s1[:, :], 1.0 / 512, ALU.mult, reads=[s1b], writes=meanb)
        self.tt("dve", mr, mean, mean, ALU.mult, reads=meanb, writes=mrb)
        self.stt(var, s2[:, :], 1.0 / 512, ALU.mult, mr, ALU.subtract, reads=[s2b] + mrb, writes=varb)
        self.act(var, var, AF.Sqrt, reads=varb, writes=varb, bias=EPS)
        P.op("dve", lambda e: e.reciprocal(out=var, in_=var), reads=varb, writes=varb)
        self.tt("dve", mr, mean, var, ALU.mult, reads=meanb + varb, writes=mrb)
        for c in range(4):
            yc, ycb = self._u32(YC32 + 2 * c), ARb[YC32 + 2 * c:YC32 + 2 * c + 2]
            self.tt("dve", yc, yc, var, ALU.mult, reads=ycb + varb, writes=ycb)
            self.tt("dve", yc, yc, mr, ALU.subtract, reads=ycb + mrb, writes=ycb)
            self.act(AR[:, yC + c, :], yc, AF.Silu, reads=ycb, writes=[ARb[yC + c]],
                     scale=self.vcol(L, "cln_g", c), bias=self.vcol(L, "cln_b", c))

        for hh in range(4):
            if hh < 2:
                pz, pzb = self.ps[6 + hh], self.psb[6 + hh]
            else:
                pz, pzb = self.bank()
            for nb in range(4):
                sl = pz[:, nb * 128:(nb + 1) * 128]
                self.mm(sl, pzb, AR[:, VLN + nb, hh * 128:(hh + 1) * 128], self.sguw[:, L, hh * 128:(hh + 1) * 128],
                        reads=[ARb[VLN + nb]], start=True, stop=False)
                bo = L * 512 + hh * 128
                self.mm(sl, pzb, self.onesb[0:1, :], self.bhl[0:1, 0, bo:bo + 128], reads=[], start=False, stop=False)
                self.mm(sl, pzb, self.onesb[0:1, :], self.bhl[0:1, 1, bo:bo + 128], reads=[], start=False, stop=True,
                        sig=(nb == 3))
            self.tt("dve", AR[:, yD + hh, :], pz[:, :], AR[:, UU + hh, :], ALU.mult, reads=[pzb, ARb[UU + hh]], writes=[ARb[yD + hh]])

        def merge_sum(m):
            t0 = TG + 4 * (m % 2)
            acc, accb = self._u32(34), ARb[34:36]
            self.tt("pool", acc, AR[:, t0, :], AR[:, t0 + 1, :], ALU.add, reads=[ARb[t0], ARb[t0 + 1]], writes=accb)
            self.tt("pool", acc, acc, AR[:, t0 + 2, :], ALU.add, reads=accb + [ARb[t0 + 2]], writes=accb)
            self.tt("pool", AR[:, MERG + m, :], acc, AR[:, t0 + 3, :], ALU.add, reads=accb + [ARb[t0 + 3]], writes=[ARb[MERG + m]])

        for m in range(8):
            sU, bU = self.W(f"L{L}_up_{m}")
            sG, bG = self.W(f"L{L}_gt_{m}")
            for g in (0, 1, 2, 3):
                if "ABCD"[g] not in BRANCHES:
                    continue
                ybase = (yA, yB, yC, yD)[g]
                pg, pgb = self.bank()
                for kc in range(KC):
                    o = g * 1024 + kc * 128
                    self.mm(pg[:, :], pgb, sG[:, o:o + 128], self.H[:, kc, :], reads=[bG, self.Hb[kc]],
                            start=(kc == 0), stop=(kc == KC - 1))
                u = 24 + 2 * (g % 2)
                gt, gtb = self._u32(u), ARb[u:u + 2]
                self.act(gt, pg[:, :], AF.Sigmoid, reads=[pgb], writes=gtb)
                pu, pub = self.bank()
                for k in range(4):
                    o = g * 512 + k * 128
                    self.mm(pu[:, :], pub, sU[:, o:o + 128], AR[:, ybase + k, :], reads=[bU, ARb[ybase + k]],
                            start=(k == 0), stop=(k == 3))
                tg = TG + g + 4 * (m % 2)
                self.tt("dve", AR[:, tg, :], pu[:, :], gt, ALU.mult, reads=[pub] + gtb, writes=[ARb[tg]])
            self.release(2)
            if m >= 1:
                merge_sum(m - 1)
        merge_sum(7)
        for hf in range(2):
            sO, bO = self.W(f"L{L}_wo_{hf}")
            for mm_ in range(4):
                m = hf * 4 + mm_
                po, pob = self.bank()
                for kc in range(KC):
                    o = kc * 512 + mm_ * 128
                    self.mm(po[:, :], pob, sO[:, o:o + 128], AR[:, MERG + kc, :], reads=[bO, ARb[MERG + kc]],
                            start=(kc == 0), stop=(kc == KC - 1))
                self.tt("dve", self.X[:, m, :], po[:, :], self.X[:, m, :], ALU.add, reads=[pob, self.Xb[m]], writes=[self.Xb[m]])
                if m >= 1:
                    self.xstat_feed(m - 1)
            self.release()
        self.xstat_feed(KC - 1)


def run(inputs, NT=S // T, NL=2, ncores=8, stages=("f1", "mix", "f2"), fin=True, trace=False):
    inp = {k: np.asarray(v, dtype=np.float32) for k, v in inputs.items()}
    host, offs = build_host_arrays(inp, NL)
    wtot = host["wall"].shape[1]
    b = Builder(NT, NL, offs, wtot, stages=stages, fin=fin)
    nc = b.build()
    SS = NT * T
    in_maps = []
    for i in range(ncores):
        m = dict(host)
        m["xT"] = np.ascontiguousarray(inp["x"][i, :SS, :].T)
        in_maps.append(m)
    res = run_bass_kernel_spmd(nc, in_maps, core_ids=list(range(ncores)), trace=trace)
    out = np.stack([np.ascontiguousarray(res.results[i]["oT"].T) for i in range(ncores)], 0)
    return out.astype(np.float32), res, b


def kernel(**inputs):
    out, _, _ = run(inputs)
    return out
```
